# Optimizing a Trainium2 kernel written in Bass

```python
import jax, jax.numpy as jnp
from jax import lax
import numpy as np

D_MODEL = 1024
BATCH = 8
SEQ = 2048
DEPTH = 2
DEC_BATCH = 128
DEC_SEQ = 4
PAST_LEN = 16384
PAGE_SIZE = 128

N_MIXERS = 2
N_CONV_LAYERS = (DEPTH + 1) // 2
N_RET_LAYERS = DEPTH // 2
CONV_WIDTH = 3
N_HEADS = 4
QK_DIM = D_MODEL
HEAD_DK = QK_DIM // N_HEADS
V_DIM = 2 * D_MODEL
HEAD_DV = V_DIM // N_HEADS
RET_IN = 2 * QK_DIM + 2 * V_DIM
CHUNK = 128
D_FF = -(-8 * D_MODEL // (3 * 256)) * 256
RMS_EPS = 1e-6
GN_EPS = 1e-6
ROPE_BASE = 10000.0

kernel_name = "hybrid_shortconv_retention_decode_step"


def rmsnorm(x, g):
    xf = x.astype(jnp.float32)
    y = xf * lax.rsqrt(jnp.mean(xf * xf, axis=-1, keepdims=True) + RMS_EPS)
    return (y * g.astype(jnp.float32)).astype(x.dtype)


def swiglu(x, w_gate, w_up, w_down):
    return (jax.nn.silu(x @ w_gate) * (x @ w_up)) @ w_down


def short_conv_mixer(x, buf, w_in, w_conv, w_out):
    L = x.shape[1]
    b, c, h = jnp.split(x @ w_in, 3, axis=-1)
    u = c * h
    full = jnp.concatenate([buf.astype(u.dtype), u], axis=1)
    y = sum(w_conv[j] * full[:, j:j + L] for j in range(CONV_WIDTH))
    out = (b * y) @ w_out
    return out, full[:, -(CONV_WIDTH - 1):]


def rotary(x, pos):
    half = x.shape[-1] // 2
    inv = ROPE_BASE ** (-jnp.arange(half, dtype=jnp.float32) / half)
    ang = pos.astype(jnp.float32)[:, None] * inv[None, :]
    cos, sin = jnp.cos(ang), jnp.sin(ang)
    xf = x.astype(jnp.float32)
    x1, x2 = xf[..., :half], xf[..., half:]
    return jnp.concatenate([x1 * cos - x2 * sin, x1 * sin + x2 * cos], axis=-1).astype(x.dtype)


def retention_chunkwise(q, k, v, s0):
    B, H, L, _ = q.shape
    C = L if L <= CHUNK else CHUNK
    n = L // C
    dt = q.dtype
    log_g = jnp.log(1.0 - 2.0 ** (-5.0 - jnp.arange(H, dtype=jnp.float32)))
    idx = jnp.arange(C, dtype=jnp.float32)
    diff = idx[:, None] - idx[None, :]
    decay = jnp.where(diff[None] >= 0, jnp.exp(jnp.maximum(diff, 0.0)[None] * log_g[:, None, None]), 0.0).astype(dt)
    cross_w = jnp.exp((idx[None] + 1.0) * log_g[:, None]).astype(dt)
    state_w = jnp.exp((C - 1.0 - idx[None]) * log_g[:, None]).astype(dt)
    chunk_decay = jnp.exp(C * log_g).astype(dt)

    def to_chunks(t):
        return jnp.moveaxis(t.reshape(B, H, n, C, t.shape[-1]), 2, 0)

    def step(S, qkv):
        qc, kc, vc = qkv
        scores = jnp.einsum('bhid,bhjd->bhij', qc, kc) * decay
        inner = jnp.einsum('bhij,bhjv->bhiv', scores, vc)
        cross = jnp.einsum('bhid,bhdv->bhiv', qc, S) * cross_w[:, :, None]
        S_new = S * chunk_decay[:, None, None] + jnp.einsum('bhjd,bhjv->bhdv', kc * state_w[:, :, None], vc)
        return S_new, inner + cross

    S, o = lax.scan(step, s0.astype(dt), (to_chunks(q), to_chunks(k), to_chunks(v)))
    o = jnp.moveaxis(o, 0, 2).reshape(B, H, L, v.shape[-1])
    return o, S


def retention_mixer(x, s0, pos, w_in, gn_g, w_out):
    B, L, _ = x.shape
    z = x @ w_in
    q, k, v, g = jnp.split(z, [QK_DIM, 2 * QK_DIM, 2 * QK_DIM + V_DIM], axis=-1)
    q = q.reshape(B, L, N_HEADS, HEAD_DK).transpose(0, 2, 1, 3)
    k = k.reshape(B, L, N_HEADS, HEAD_DK).transpose(0, 2, 1, 3)
    v = v.reshape(B, L, N_HEADS, HEAD_DV).transpose(0, 2, 1, 3)
    q = rotary(q, pos)
    k = rotary(k, pos) * (HEAD_DK ** -0.5)
    o, S = retention_chunkwise(q, k, v, s0)
    of = o.astype(jnp.float32)
    mu = jnp.mean(of, axis=-1, keepdims=True)
    var = jnp.mean(jnp.square(of - mu), axis=-1, keepdims=True)
    of = (of - mu) * lax.rsqrt(var + GN_EPS)
    of = of.transpose(0, 2, 1, 3).reshape(B, L, V_DIM) * gn_g.astype(jnp.float32)
    out = (jax.nn.silu(g) * of.astype(x.dtype)) @ w_out
    return out, S


def trunk(x, conv_bufs, ret_states, pos, norm_mix, norm_ffn, conv_w_in, conv_w, conv_w_out,
          ret_w_in, ret_gn, ret_w_out, ffn_w_gate, ffn_w_up, ffn_w_down, final_norm):
    new_conv, new_ret = [], []
    for i in range(DEPTH):
        j = i // N_MIXERS
        h = rmsnorm(x, norm_mix[i])
        if i % N_MIXERS == 0:
            m, st = short_conv_mixer(h, conv_bufs[j], conv_w_in[j], conv_w[j], conv_w_out[j])
            new_conv.append(st)
        else:
            m, st = retention_mixer(h, ret_states[j], pos, ret_w_in[j], ret_gn[j], ret_w_out[j])
            new_ret.append(st)
        x = x + m
        x = x + swiglu(rmsnorm(x, norm_ffn[i]), ffn_w_gate[i], ffn_w_up[i], ffn_w_down[i])
    return rmsnorm(x, final_norm), jnp.stack(new_conv), jnp.stack(new_ret)


def setup_inputs(seed: int = 0) -> dict:
    key = jax.random.key(seed)
    ks = jax.random.split(key, 16)
    nrm = jax.random.normal
    D = D_MODEL
    return {
        "x_prompt": nrm(ks[0], (BATCH, SEQ, D), jnp.float32),
        "x_sample": nrm(ks[1], (DEC_BATCH, DEC_SEQ, D), jnp.float32),
        "state_conv": nrm(ks[2], (N_CONV_LAYERS, DEC_BATCH, CONV_WIDTH - 1, D), jnp.float32),
        "state_ret": 0.5 * nrm(ks[3], (N_RET_LAYERS, DEC_BATCH, N_HEADS, HEAD_DK, HEAD_DV), jnp.float32),
        "norm_mix": 1.0 + 0.02 * nrm(ks[4], (DEPTH, D), jnp.float32),
        "norm_ffn": 1.0 + 0.02 * nrm(ks[5], (DEPTH, D), jnp.float32),
        "conv_w_in": nrm(ks[6], (N_CONV_LAYERS, D, 3 * D), jnp.float32) * D ** -0.5,
        "conv_w": nrm(ks[7], (N_CONV_LAYERS, CONV_WIDTH, D), jnp.float32) * CONV_WIDTH ** -0.5,
        "conv_w_out": nrm(ks[8], (N_CONV_LAYERS, D, D), jnp.float32) * D ** -0.5,
        "ret_w_in": nrm(ks[9], (N_RET_LAYERS, D, RET_IN), jnp.float32) * D ** -0.5,
        "ret_gn": 1.0 + 0.02 * nrm(ks[10], (N_RET_LAYERS, V_DIM), jnp.float32),
        "ret_w_out": nrm(ks[11], (N_RET_LAYERS, V_DIM, D), jnp.float32) * V_DIM ** -0.5,
        "ffn_w_gate": nrm(ks[12], (DEPTH, D, D_FF), jnp.float32) * D ** -0.5,
        "ffn_w_up": nrm(ks[13], (DEPTH, D, D_FF), jnp.float32) * D ** -0.5,
        "ffn_w_down": nrm(ks[14], (DEPTH, D_FF, D), jnp.float32) * D_FF ** -0.5,
        "final_norm": 1.0 + 0.02 * nrm(ks[15], (D,), jnp.float32),
    }


def reference(x_prompt, x_sample, state_conv, state_ret, norm_mix, norm_ffn, conv_w_in, conv_w,
              conv_w_out, ret_w_in, ret_gn, ret_w_out, ffn_w_gate, ffn_w_up, ffn_w_down, final_norm):
    Bp, Lp, _ = x_prompt.shape
    Ls = x_sample.shape[1]
    conv0 = jnp.zeros((N_CONV_LAYERS, Bp, CONV_WIDTH - 1, D_MODEL), x_prompt.dtype)
    ret0 = jnp.zeros((N_RET_LAYERS, Bp, N_HEADS, HEAD_DK, HEAD_DV), x_prompt.dtype)
    pos_p = jnp.arange(Lp, dtype=jnp.float32)
    pos_s = PAST_LEN + jnp.arange(Ls, dtype=jnp.float32)
    w = (norm_mix, norm_ffn, conv_w_in, conv_w, conv_w_out, ret_w_in, ret_gn, ret_w_out,
         ffn_w_gate, ffn_w_up, ffn_w_down, final_norm)
    y_prompt, conv_prompt, ret_prompt = trunk(x_prompt, conv0, ret0, pos_p, *w)
    y_sample, conv_sample, ret_sample = trunk(x_sample, state_conv, state_ret, pos_s, *w)
    return (y_prompt, y_sample, conv_prompt, conv_sample, ret_prompt, ret_sample)
```

```python
import contextlib
import numpy as np
import concourse.bass as bass
import concourse.mybir as mybir
from concourse.bass_utils import run_bass_kernel_spmd

F32 = mybir.dt.float32
BF16 = mybir.dt.bfloat16
AF = mybir.ActivationFunctionType
ALU = mybir.AluOpType

D = 1024
NT = 2112
TT = [(0, 512), (512, 512), (1024, 512), (1536, 512), (2048, 64)]
DFF = 2816
NH = 4
RMS_EPS = 1e-6
GN_EPS = 1e-6
ENGS = ("pe", "act", "dve", "pool", "sp")
NSLOT = 6


class Buf:
    __slots__ = ("name", "w", "r", "fence")
    FENCE = None

    def __init__(self, name=""):
        self.name = name
        self.w = None
        self.r = []
        self.fence = Buf.FENCE


class Op:
    __slots__ = ("eng", "fn", "deps", "dma", "sig", "sem", "val", "pre", "name")


class Kern:
    def __init__(self, nc, n_dma_sems=16):
        self.nc = nc
        self.ops = []
        self.n_dma_sems = n_dma_sems
        self.last = {e: None for e in ENGS}
        self.dmas = []

    def op(self, eng, fn, reads=(), writes=(), dma=False, name="", extra=()):
        o = Op()
        o.eng, o.fn, o.dma, o.name = eng, fn, dma, name
        o.sig, o.sem, o.val, o.pre = False, None, 0, None
        deps = set(extra)
        for b in reads:
            if b.w is not None:
                deps.add(b.w)
            if b.fence is not None:
                deps.update(b.fence)
        for b in writes:
            if b.w is not None:
                deps.add(b.w)
            deps.update(b.r)
            if b.fence is not None:
                deps.update(b.fence)
                b.fence = None
        if eng == "pe":
            deps = {d for d in deps if not (d.eng == "pe" and not d.dma)}
        o.deps = deps
        for b in reads:
            b.r.append(o)
        for b in writes:
            b.w = o
            b.r = []
        self.ops.append(o)
        if dma:
            self.dmas.append(o)
        else:
            self.last[eng] = o
        return o

    def fence(self):
        fr = [o for o in self.last.values() if o is not None] + list(self.dmas)
        Buf.FENCE = fr

    def barrier(self):
        ex = [o for o in self.last.values() if o is not None] + list(self.dmas)
        self.dmas = []
        saved = dict(self.last)
        for e in ENGS:
            self.op(e, lambda h: None, extra=ex, name="barrier")
        self.last = saved

    def emit(self):
        nc = self.nc
        ops = self.ops
        for o in ops:
            if o.dma:
                o.sig = True
            for d in o.deps:
                d.sig = True
        with contextlib.ExitStack() as st:
            esem = {e: st.enter_context(nc.semaphore(f"s_{e}")) for e in ENGS if e != "sp"}
            nd = self.n_dma_sems
            dsem = {e: [st.enter_context(nc.semaphore(f"s_dma_{e}{i}")) for i in range(nd)] for e in ("sp", "pool", "act")}
            cnt = {e: 0 for e in ENGS}
            ndma = {e: 0 for e in ENGS}
            for o in ops:
                if o.dma:
                    k = ndma[o.eng]
                    ndma[o.eng] += 1
                    o.sem = dsem[o.eng][k % nd]
                    o.val = 16 * (k // nd + 1)
                    o.pre = (o.sem, o.val - 16) if o.val > 16 else None
                elif o.sig:
                    assert o.eng != "sp"
                    cnt[o.eng] += 1
                    o.sem = esem[o.eng]
                    o.val = cnt[o.eng]
            streams = {e: [o for o in ops if o.eng == e] for e in ENGS}
            self.stats = {e: len(streams[e]) for e in ENGS}
            self.stats["sig"] = dict(cnt)
            block = st.enter_context(nc.Block())

            def run(e, h):
                known = {}
                for o in streams[e]:
                    need = {}
                    for d in o.deps:
                        k = id(d.sem)
                        if need.get(k, (None, 0))[1] < d.val:
                            need[k] = (d.sem, d.val)
                    if o.pre is not None:
                        k = id(o.pre[0])
                        if need.get(k, (None, 0))[1] < o.pre[1]:
                            need[k] = o.pre
                    for k, (s, v) in need.items():
                        if known.get(k, 0) < v:
                            h.wait_ge(s, v)
                            known[k] = v
                    ins = o.fn(h)
                    if o.sig:
                        assert ins is not None, o.name
                        ins.then_inc(o.sem, 16 if o.dma else 1)

            @block.tensor
            def _(h):
                run("pe", h)

            @block.scalar
            def _(h):
                run("act", h)

            @block.vector
            def _(h):
                run("dve", h)

            @block.gpsimd
            def _(h):
                run("pool", h)

            @block.sync
            def _(h):
                run("sp", h)


def _consts():
    half = 128
    inv = 10000.0 ** (-(np.arange(half, dtype=np.float64) / half))
    pos = np.concatenate([np.arange(2048, dtype=np.float64), np.tile(16384.0 + np.arange(4, dtype=np.float64), 16)])
    ang = pos[None, :] * inv[:, None]
    cos = np.cos(ang)
    sin = np.sin(ang)
    e = np.concatenate([np.arange(2048, dtype=np.float64), np.tile(np.arange(4, dtype=np.float64), 16)]) + 1.0
    g = np.array([1.0 - 2.0 ** (-5.0 - h) for h in range(NH)], dtype=np.float64)
    lg = np.log(g.astype(np.float32)).astype(np.float64)
    ropeq = np.zeros((NH, 2, 128, NT), np.float32)
    ropek = np.zeros((NH, 2, 128, NT), np.float32)
    for h in range(NH):
        sq = np.exp(e * lg[h])[None, :]
        sk = np.exp(-e * lg[h])[None, :] * (256.0 ** -0.5)
        ropeq[h, 0] = cos * sq
        ropeq[h, 1] = sin * sq
        ropek[h, 0] = cos * sk
        ropek[h, 1] = sin * sk
    j = np.arange(128)
    maskT = (j[:, None] <= j[None, :]).astype(np.float32)
    js = np.arange(64)
    smask = ((js[:, None] // 4 == js[None, :] // 4) & (js[:, None] <= js[None, :])).astype(np.float32)
    kmcol = np.zeros((64, NH * 16), np.float32)
    for h in range(NH):
        for s in range(16):
            kmcol[4 * s:4 * s + 4, h * 16 + s] = np.exp(4.0 * lg[h])
    g4 = [float(np.exp(4.0 * lg[h])) for h in range(NH)]
    g2048 = [float(np.exp(2048.0 * lg[h])) for h in range(NH)]
    return dict(ropeq=ropeq, ropek=ropek, maskT=maskT, smask=smask, kmcol=kmcol,
                ident=np.eye(128, dtype=np.float32), ones=np.ones((128, 128), np.float32)), g4, g2048


def build_nc(g4, g2048, stop_after=None):
    nc = bass.Bass("TRN2", target_bir_lowering=False)

    def din(name, shape):
        return nc.dram_tensor(name, list(shape), F32, kind="ExternalInput").ap()

    def dout(name, shape):
        return nc.dram_tensor(name, list(shape), F32, kind="ExternalOutput").ap()

    xin = din("xin", [NT, D])
    sconv = din("sconv", [32, D])
    sret = din("sret", [16, NH, 256, 512])
    cols_d = din("cols", [128, 80])
    conv_w_in = din("conv_w_in", [D, 3 * D])
    conv_w_out = din("conv_w_out", [D, D])
    ret_w_in = din("ret_w_in", [D, 6 * D])
    ret_w_out = din("ret_w_out", [2 * D, D])
    ffn_w_gate = din("ffn_w_gate", [2, D, DFF])
    ffn_w_up = din("ffn_w_up", [2, D, DFF])
    ffn_w_down = din("ffn_w_down", [2, DFF, D])
    ropeq_d = din("ropeq", [NH, 2, 128, NT])
    ropek_d = din("ropek", [NH, 2, 128, NT])
    maskT_d = din("maskT", [128, 128])
    smask_d = din("smask", [64, 64])
    kmcol_d = din("kmcol", [64, NH * 16])
    ident_d = din("ident", [128, 128])
    ones_d = din("ones", [128, 128])
    y_d = dout("y", [NT, D])
    convo_d = dout("convo", [34, D])
    retp_d = dout("retp", [NH, 256, 512])
    rets_d = dout("rets", [16, NH, 256, 512])

    Buf.FENCE = None
    K = Kern(nc)
    with contextlib.ExitStack() as top:
        uniq = {"n": 0}

        def T(st, name, shape, dt):
            uniq["n"] += 1
            return st.enter_context(nc.sbuf_tensor(f"sb{uniq['n']}_{name}", list(shape), dt))

        def PS(name, shape, dt):
            return top.enter_context(nc.psum_tensor("ps_" + name, list(shape), dt))

        xT = T(top, "xT", [128, 8, NT], F32)
        hT = T(top, "hT", [128, 8, NT], BF16)
        ring = [T(top, f"ring{i}", [128, 2048], BF16) for i in range(NSLOT)]
        ident = T(top, "ident", [128, 128], F32)
        identb = T(top, "identb", [128, 128], BF16)
        ones = T(top, "ones", [128, 128], F32)
        cols = T(top, "colsb", [128, 80], F32)
        maskT = T(top, "maskT", [128, 128], F32)
        smask = T(top, "smask", [64, 64], F32)
        kmcol = T(top, "kmcol", [64, NH * 16], F32)
        acc2 = [T(top, f"acc{i}", [128, 512], F32) for i in range(2)]
        sq = [T(top, f"sq{i}", [128, 512], F32) for i in range(2)]
        rs2 = [T(top, f"rs{i}", [128, 512], F32) for i in range(2)]

        NS0 = 2
        sS0f = [T(top, f"sS0f{i}", [128, 2, 512], F32) for i in range(NS0)]
        b_sS0f = [Buf() for _ in range(NS0)]
        sk = [T(top, f"sk{i}", [64, 256], BF16) for i in range(2)]
        sv = [T(top, f"sv{i}", [64, 512], BF16) for i in range(2)]
        b_skv = [Buf(), Buf()]
        kmk = [T(top, f"kmk{i}", [64, 256], BF16) for i in range(2)]
        b_kmk = [Buf(), Buf()]

        class BG:
            def __init__(self):
                self.q = []
                self.n = 0
                self.stride = 3

            def add(self, fn):
                self.q.append(fn)

            def tick(self):
                self.n += 1
                if self.q and self.n % self.stride == 0:
                    self.q.pop(0)()

            def drain(self):
                while self.q:
                    self.q.pop(0)()
        bgq = BG()

        b_x = [[Buf(f"x{c}_{t}") for t in range(5)] for c in range(8)]
        b_h = [[Buf(f"h{c}_{t}") for t in range(5)] for c in range(8)]
        b_ring = [Buf(f"ring{i}") for i in range(NSLOT)]
        b_const = Buf("const")
        b_acc2, b_rs2 = [Buf(), Buf()], [Buf(), Buf()]
        b_sq = [Buf("sq0"), Buf("sq1")]

        pbank = [PS(f"pb{i}", [128, 512], F32) for i in range(7)]
        b_pbank = [Buf(f"pb{i}") for i in range(7)]
        ptb = PS("ptb", [128, 1024], BF16)
        _bp = Buf("ptb")
        b_ptb = [_bp, _bp]
        ptbs = {"n": 0}
        pstate = {"next": 0, "avail": [0, 1, 2, 3, 4, 5, 6]}

        def getbank():
            a = pstate["avail"]
            i = a[pstate["next"] % len(a)]
            pstate["next"] += 1
            return pbank[i], b_pbank[i]

        ring_state = {"next": 0}
        x_ops = []

        def wload(src_ap, shape3):
            i = ring_state["next"] % NSLOT
            ring_state["next"] += 1
            a, b = shape3
            view = ring[i][:, 0:a * b].rearrange("p (a b) -> p a b", a=a)
            ex = ()
            if ring_state["next"] == 1:
                ex = tuple(x_ops)
            K.op("pool", lambda h: h.dma_start(out=view, in_=src_ap), writes=[b_ring[i]], dma=True, name="wload", extra=ex)
            return view, b_ring[i]

        def wcols(w2d, col0, ncol=256):
            src = w2d.rearrange("(kc p) n -> p kc n", p=128)[:, :, col0:col0 + ncol]
            return wload(src, (8, ncol))

        def wrows(w2d, row0):
            src = w2d[row0:row0 + 256, :].rearrange("(r p) n -> p r n", p=128)
            return wload(src, (2, 1024))

        for (dst, src, nm) in ((ident, ident_d, "ident"), (ones, ones_d, "ones"), (cols, cols_d, "cols"),
                               (maskT, maskT_d, "maskT"), (smask, smask_d, "smask"), (kmcol, kmcol_d, "kmcol")):
            K.op("sp" if nm == "ident" else "act", lambda h, dst=dst, src=src: h.dma_start(out=dst[:], in_=src),
                 writes=[b_const], dma=True, name=nm)
        K.op("dve", lambda h: h.tensor_copy(out=identb[:], in_=ident[:]), reads=[b_const], writes=[b_const])
        EPS_R = None

        def gcol(gi, c):
            return cols[:, gi * 8 + c:gi * 8 + c + 1]

        def wccol(j, c):
            return cols[:, 40 + j * 8 + c:40 + j * 8 + c + 1]

        def gncol(m):
            return cols[:, 64 + m:64 + m + 1]

        epsc = T(top, "epsc", [128, 2], F32)
        K.op("dve", lambda h: h.memset(epsc[:, 0:1], RMS_EPS), writes=[b_const])
        K.op("dve", lambda h: h.memset(epsc[:, 1:2], GN_EPS), reads=[b_const], writes=[b_const])

        def mm_group(out_ap, pairs, reads, bbuf, name="mm"):
            n = len(pairs)

            def fn(h):
                ins = None
                for i, (l, r) in enumerate(pairs):
                    ins = h.matmul(out_ap, lhsT=l, rhs=r, start=(i == 0), stop=(i == n - 1))
                return ins
            return K.op("pe", fn, reads=reads, writes=[bbuf], name=name)

        nstate = {"n": 0}
        pending_h = {}

        def need_h(tt):
            f = pending_h.pop(tt, None)
            if f is not None:
                f()


        def norm_tile(gi, tt, final_cb=None, defer=False):
            t0, N = TT[tt]
            p = nstate["n"] % 2
            nstate["n"] += 1
            acc_, rs_, bacc_, brs_ = acc2[p], rs2[p], b_acc2[p], b_rs2[p]
            for c in range(8):
                if c == 0:
                    K.op("act", lambda h: h.activation(out=acc_[:, :N], in_=xT[:, 0, t0:t0 + N], func=AF.Square),
                         reads=[b_x[0][tt]], writes=[bacc_], name="sq0")
                else:
                    s_ = c % 2
                    K.op("act", lambda h, c=c, s_=s_: h.activation(out=sq[s_][:, :N], in_=xT[:, c, t0:t0 + N], func=AF.Square),
                         reads=[b_x[c][tt]], writes=[b_sq[s_]], name="sq")
                    K.op("dve", lambda h, s_=s_: h.tensor_tensor(out=acc_[:, :N], in0=acc_[:, :N], in1=sq[s_][:, :N], op=ALU.add),
                         reads=[b_sq[s_], bacc_], writes=[bacc_], name="sqadd")
            def part2():
                norm_part2(gi, tt, t0, N, acc_, rs_, bacc_, brs_, final_cb)
            if defer:
                return part2
            part2()

        def norm_part2(gi, tt, t0, N, acc_, rs_, bacc_, brs_, final_cb):
            bank, bb = getbank()
            K.op("pe", lambda h: h.matmul(bank[:, :N], lhsT=ones[:, :], rhs=acc_[:, :N], start=True, stop=True),
                 reads=[bacc_, b_const], writes=[bb], name="onesmm")
            K.op("act", lambda h: h.activation(out=rs_[:, :N], in_=bank[:, :N], func=AF.Ln, bias=epsc[:, 0:1], scale=1.0 / D),
                 reads=[bb, b_const], writes=[brs_], name="rln")
            K.op("act", lambda h: h.activation(out=rs_[:, :N], in_=rs_[:, :N], func=AF.Exp, scale=-0.5),
                 reads=[brs_], writes=[brs_], name="rexp")
            if final_cb is not None:
                final_cb(tt, t0, N, rs_, brs_)
                return
            for c in range(8):
                K.op("dve", lambda h, c=c: h.scalar_tensor_tensor(
                    out=hT[:, c, t0:t0 + N], in0=xT[:, c, t0:t0 + N], scalar=gcol(gi, c), in1=rs_[:, :N],
                    op0=ALU.mult, op1=ALU.mult),
                    reads=[b_x[c][tt], brs_, b_const], writes=[b_h[c][tt]], name="hnorm")

        def residual_add(bank, bb, mo, tt, t0, N, eng="dve"):
            K.op(eng, lambda h: h.tensor_tensor(out=xT[:, mo, t0:t0 + N], in0=xT[:, mo, t0:t0 + N], in1=bank[:, :N], op=ALU.add),
                 reads=[bb, b_x[mo][tt]], writes=[b_x[mo][tt]], name="resadd")

        with contextlib.ExitStack() as st:
            NX = 3
            xst2 = [T(st, f"xst{i}", [128, 2, D], F32) for i in range(NX)]
            b_xst2 = [Buf(f"xst{i}") for i in range(NX)]
            pend0 = [None]
            for tb in range(17):
                rows = 128 if tb < 16 else 64
                tt = tb // 4
                s2 = (tb // 2) % NX
                xs_ = xst2[s2][:, tb % 2, :]
                b_xs = b_xst2[s2]
                if tb % 2 == 0:
                    if tb < 16:
                        src = xin[tb * 128:tb * 128 + 256, :].rearrange("(j p) d -> p j d", p=128)
                        x_ops.append(K.op("sp" if (tb // 2) % 2 == 0 else "pool", lambda h, s2=s2, src=src: h.dma_start(out=xst2[s2][:, :, :], in_=src),
                                          writes=[b_xs], dma=True, name="xload"))
                    else:
                        x_ops.append(K.op("sp", lambda h, s2=s2: h.dma_start(out=xst2[s2][:64, 0, :], in_=xin[2048:2112, :]),
                                          writes=[b_xs], dma=True, name="xload"))
                for half in range(2):
                    bank, bb = getbank()

                    def tr(h, bank=bank, xs_=xs_, rows=rows, half=half):
                        ins = None
                        for q in range(4):
                            c = half * 4 + q
                            ins = h.transpose(out=bank[:, q * 128:q * 128 + rows], in_=xs_[:rows, c * 128:(c + 1) * 128],
                                              identity=ident[:rows, :rows])
                        return ins
                    K.op("pe", tr, reads=[b_xs, b_const], writes=[bb], name="xtr")
                    eng = "act"

                    def ev(h, bank=bank, tb=tb, rows=rows, half=half, eng=eng):
                        src = bank[:, :].rearrange("p (q t) -> p q t", q=4)[:, :, :rows]
                        dst = xT[:, half * 4:half * 4 + 4, tb * 128:tb * 128 + rows]
                        if eng == "act":
                            return h.copy(out=dst, in_=src)
                        return h.tensor_copy(out=dst, in_=src)
                    K.op(eng, ev, reads=[bb], writes=[b_x[half * 4 + q][tt] for q in range(4)], name="xev")
                if tb % 4 == 3 or tb == 16:
                    if pend0[0] is not None:
                        pend0[0]()
                        pend0[0] = None
                    p2 = norm_tile(0, tt, defer=True)
                    if tt >= 3:
                        pending_h[tt] = p2
                    else:
                        pend0[0] = p2
        K.fence()

        def conv_stage():
            with contextlib.ExitStack() as st:
                byT = T(st, "byT", [128, 8, NT], BF16)
                b_by = [[Buf() for _ in range(5)] for _ in range(8)]
                uf = [T(st, "uf0", [128, 2050], F32)] * 2
                us = [T(st, "us0", [128, 16, 6], F32)] * 2
                b_u = [[Buf() for _ in range(6)]] * 2
                tcp = [T(st, "tcp0", [128, 512], F32)] * 2
                b_tcp = [Buf()] * 2
                yt = [T(st, "yt0", [128, 512], F32)] * 2
                b_yt = [Buf()] * 2
                cst = T(st, "cst", [128, 8, 32], F32)
                cso = T(st, "cso", [128, 8, 34], F32)
                csr = T(st, "csr", [34, D], F32)
                scs = csr
                b_cst, b_cso, b_csr = Buf(), Buf(), Buf()
                b_scs = b_csr

                K.op("sp", lambda h: h.dma_start(out=scs[:32, :], in_=sconv), writes=[b_scs], dma=True)
                bank, bb = getbank()

                def trs(h, bank=bank):
                    ins = None
                    for c in range(8):
                        ins = h.transpose(out=bank[:, c * 32:(c + 1) * 32], in_=scs[:32, c * 128:(c + 1) * 128], identity=ident[:32, :32])
                    return ins
                K.op("pe", trs, reads=[b_scs, b_const], writes=[bb])
                K.op("act", lambda h, bank=bank: h.copy(out=cst[:, :, :], in_=bank[:, 0:256].rearrange("p (c s) -> p c s", c=8)),
                     reads=[bb], writes=[b_cst])
                cnt = 0
                for mp in range(4):
                    wc_, bwc = wcols(conv_w_in, D + mp * 256)
                    wh_, bwh = wcols(conv_w_in, 2 * D + mp * 256)
                    wb_, bwb = wcols(conv_w_in, mp * 256)
                    for q in range(2):
                        m = 2 * mp + q
                        ui = m % 2
                        ufm, usm, bu = uf[ui], us[ui], b_u[ui]
                        K.op("dve", lambda h, ufm=ufm: h.memset(ufm[:, 0:2], 0.0), writes=[bu[5]])
                        K.op("dve", lambda h, usm=usm, m=m: h.tensor_copy(out=usm[:, :, 0:2], in_=cst[:, m, :].rearrange("p (s j) -> p s j", j=2)),
                             reads=[b_cst], writes=[bu[4]])
                        for tt, (t0, N) in enumerate(TT):
                            need_h(tt)
                            sl = slice(q * 128, (q + 1) * 128)
                            bc, bbc = getbank()
                            mm_group(bc[:, :N], [(wc_[:, kc, sl], hT[:, kc, t0:t0 + N]) for kc in range(8)],
                                     [bwc] + [b_h[kc][tt] for kc in range(8)], bbc)
                            bh, bbh = getbank()
                            mm_group(bh[:, :N], [(wh_[:, kc, sl], hT[:, kc, t0:t0 + N]) for kc in range(8)],
                                     [bwh] + [b_h[kc][tt] for kc in range(8)], bbh)
                            bbk, bbb = getbank()
                            mm_group(bbk[:, :N], [(wb_[:, kc, sl], hT[:, kc, t0:t0 + N]) for kc in range(8)],
                                     [bwb] + [b_h[kc][tt] for kc in range(8)], bbb)
                            ti = cnt % 2
                            cnt += 1
                            K.op("act", lambda h, bc=bc, N=N, ti=ti: h.copy(out=tcp[ti][:, :N], in_=bc[:, :N]),
                                 reads=[bbc], writes=[b_tcp[ti]])
                            if tt < 4:
                                udst = ufm[:, 2 + t0:2 + t0 + N]
                                K.op("dve", lambda h, udst=udst, bh=bh, N=N, ti=ti: h.tensor_tensor(out=udst, in0=tcp[ti][:, :N], in1=bh[:, :N], op=ALU.mult),
                                     reads=[b_tcp[ti], bbh], writes=[bu[tt]])
                                u0, u1, u2 = ufm[:, t0:t0 + N], ufm[:, t0 + 1:t0 + 1 + N], ufm[:, t0 + 2:t0 + 2 + N]
                                ydst = yt[ti][:, :N]
                                rd = [bu[tt], bu[tt - 1] if tt > 0 else bu[5], b_const]
                            else:
                                udst = usm[:, :, 2:6]
                                K.op("dve", lambda h, udst=udst, bh=bh, ti=ti: h.tensor_tensor(
                                    out=udst, in0=tcp[ti][:, :64].rearrange("p (s t) -> p s t", t=4),
                                    in1=bh[:, :64].rearrange("p (s t) -> p s t", t=4), op=ALU.mult),
                                    reads=[b_tcp[ti], bbh, bu[4]], writes=[bu[4]])
                                u0, u1, u2 = usm[:, :, 0:4], usm[:, :, 1:5], usm[:, :, 2:6]
                                ydst = yt[ti][:, :64].rearrange("p (s t) -> p s t", t=4)
                                rd = [bu[4], b_const]
                            K.op("dve", lambda h, ydst=ydst, u2=u2, m=m: h.tensor_scalar(out=ydst, in0=u2, scalar1=wccol(2, m), scalar2=None, op0=ALU.mult),
                                 reads=rd, writes=[b_yt[ti]])
                            K.op("dve", lambda h, ydst=ydst, u1=u1, m=m: h.scalar_tensor_tensor(out=ydst, in0=u1, scalar=wccol(1, m), in1=ydst, op0=ALU.mult, op1=ALU.add),
                                 reads=rd + [b_yt[ti]], writes=[b_yt[ti]])
                            K.op("dve", lambda h, ydst=ydst, u0=u0, m=m: h.scalar_tensor_tensor(out=ydst, in0=u0, scalar=wccol(0, m), in1=ydst, op0=ALU.mult, op1=ALU.add),
                                 reads=rd + [b_yt[ti]], writes=[b_yt[ti]])
                            K.op("dve", lambda h, bbk=bbk, ti=ti, m=m, t0=t0, N=N: h.tensor_tensor(out=byT[:, m, t0:t0 + N], in0=yt[ti][:, :N], in1=bbk[:, :N], op=ALU.mult),
                                 reads=[b_yt[ti], bbb], writes=[b_by[m][tt]])
                        K.op("act", lambda h, ufm=ufm, m=m: h.copy(out=cso[:, m, 0:2], in_=ufm[:, 2048:2050]),
                             reads=[bu[3]], writes=[b_cso])
                        K.op("act", lambda h, usm=usm, m=m: h.copy(out=cso[:, m, 2:34].rearrange("p (s j) -> p s j", j=2), in_=usm[:, :, 4:6]),
                             reads=[bu[4]], writes=[b_cso])
                for half in range(2):
                    bank, bb = getbank()

                    def trc(h, bank=bank, half=half):
                        ins = None
                        for q in range(4):
                            c = half * 4 + q
                            ins = h.transpose(out=bank[:34, q * 128:(q + 1) * 128], in_=cso[:, c, :], identity=ident[:, :])
                        return ins
                    K.op("pe", trc, reads=[b_cso, b_const], writes=[bb])
                    K.op("act", lambda h, bank=bank, half=half: h.copy(out=csr[:34, half * 512:(half + 1) * 512], in_=bank[:34, :]),
                         reads=[bb], writes=[b_csr])
                K.op("sp", lambda h: h.dma_start(out=convo_d, in_=csr[:34, :]), reads=[b_csr], dma=True, name="convo")
                wos = [wcols(conv_w_out, mp * 256) for mp in range(4)]
                pendc = [None]
                for tt, (t0, N) in enumerate(TT):
                    for mo in range(8):
                        wo_, bwo = wos[mo // 2]
                        sl = slice((mo % 2) * 128, (mo % 2) * 128 + 128)
                        bank, bb = getbank()
                        mm_group(bank[:, :N], [(wo_[:, kc, sl], byT[:, kc, t0:t0 + N]) for kc in range(8)],
                                 [bwo] + [b_by[kc][tt] for kc in range(8)], bb)
                        residual_add(bank, bb, mo, tt, t0, N)
                        if mo == 3 and pendc[0] is not None:
                            pendc[0]()
                            pendc[0] = None
                    pendc[0] = norm_tile(1, tt, defer=True)
                pending_h[4] = pendc[0]
                K.fence()

        def ffn_stage(li):
            wg2, wu2, wd2 = ffn_w_gate[li], ffn_w_up[li], ffn_w_down[li]
            with contextlib.ExitStack() as st:
                actT = T(st, "actT", [128, 8, NT], BF16)
                b_a = [[Buf() for _ in range(5)] for _ in range(8)]
                sl_ = [T(st, f"silu{i}", [128, 512], F32) for i in range(2)]
                b_sl = [Buf(), Buf()]
                fcb = make_final_cb(st) if li == 1 else None
                cnt = 0
                for (f0, G) in ((0, 6), (6, 8), (14, 8)):
                    for fp in range(G // 2):
                        f = f0 + 2 * fp
                        wg_, bwg = wcols(wg2, f * 128)
                        wu_, bwu = wcols(wu2, f * 128)
                        for q in range(2):
                            fl = 2 * fp + q
                            sl = slice(q * 128, (q + 1) * 128)
                            for tt, (t0, N) in enumerate(TT):
                                need_h(tt)
                                bg, bbg = getbank()
                                mm_group(bg[:, :N], [(wg_[:, kc, sl], hT[:, kc, t0:t0 + N]) for kc in range(8)],
                                         [bwg] + [b_h[kc][tt] for kc in range(8)], bbg)
                                bu_, bbu = getbank()
                                mm_group(bu_[:, :N], [(wu_[:, kc, sl], hT[:, kc, t0:t0 + N]) for kc in range(8)],
                                         [bwu] + [b_h[kc][tt] for kc in range(8)], bbu)
                                ti = cnt % 2
                                cnt += 1
                                K.op("act", lambda h, bg=bg, N=N, ti=ti: h.activation(out=sl_[ti][:, :N], in_=bg[:, :N], func=AF.Silu),
                                     reads=[bbg], writes=[b_sl[ti]], name="silu")
                                K.op("dve", lambda h, bu_=bu_, N=N, ti=ti, fl=fl, t0=t0: h.tensor_tensor(
                                    out=actT[:, fl, t0:t0 + N], in0=sl_[ti][:, :N], in1=bu_[:, :N], op=ALU.mult),
                                    reads=[b_sl[ti], bbu], writes=[b_a[fl][tt]], name="actmul")
                                bgq.tick()
                    wds = [wrows(wd2, (f0 + 2 * j) * 128) for j in range(G // 2)]
                    lastg = (f0 == 14)

                    def down(mo, tt):
                        t0, N = TT[tt]
                        bank, bb = getbank()
                        pairs = [(wds[fl // 2][0][:, fl % 2, mo * 128:(mo + 1) * 128], actT[:, fl, t0:t0 + N]) for fl in range(G)]
                        mm_group(bank[:, :N], pairs, [w[1] for w in wds] + [b_a[fl][tt] for fl in range(G)], bb)
                        residual_add(bank, bb, mo, tt, t0, N)
                    if not lastg:
                        for mo in range(8):
                            for tt in range(5):
                                down(mo, tt)
                    else:
                        pendf = None
                        for tt in range(5):
                            for mo in range(8):
                                down(mo, tt)
                                if mo == 3 and pendf is not None:
                                    pendf()
                                    pendf = None
                            if li == 0:
                                pendf = norm_tile(2, tt, defer=True)
                            else:
                                pendf = norm_tile(4, tt, final_cb=fcb, defer=True)
                        if li == 0:
                            pending_h[4] = pendf
                        else:
                            pendf()
                bgq.drain()
                if li == 0:
                    K.fence()
                else:
                    K.barrier()

        def ret_stage():
            wq_all = ret_w_in
            HT = 1088
            halves = ((0, (0, 1)), (1024, (2, 3, 4)))
            with contextlib.ExitStack() as st:
                QK = T(st, "QK", [128, 4, HT], BF16)
                b_qk = [[Buf() for _ in range(9)] for _ in range(4)]
                ktok = T(st, "ktok", [128, 9, 256], BF16)
                b_kt = [Buf() for _ in range(9)]
                vtok = T(st, "vtok", [128, 9, 512], BF16)
                b_vt = [Buf() for _ in range(9)]
                tmp = [T(st, f"rtmp{i}", [128, 512], F32) for i in range(5)]
                b_tmp = [Buf() for _ in range(5)]
                tab = [T(st, f"tab{i}", [128, 512], F32) for i in range(2)]
                b_tab = [Buf() for _ in range(2)]
                PT = [T(st, f"PT{i}", [128, 128], BF16) for i in range(2)]
                b_PT = [Buf(), Buf()]
                PTs = T(st, "PTs", [64, 64], BF16)
                b_PTs = Buf()
                on = [T(st, f"on{i}", [128, 512], BF16) for i in range(2)]
                b_on = [Buf(), Buf()]
                stats = [T(st, f"stats{i}", [128, 8], F32) for i in range(2)]
                mv = [T(st, f"mv{i}", [128, 8], F32) for i in range(2)]
                b_stats, b_mv = [Buf(), Buf()], [Buf(), Buf()]
                Sb = T(st, "Sb", [128, 2, 512], BF16)
                b_Sb = Buf()
                S0b = [T(st, f"S0b{i}", [128, 2, 512], BF16) for i in range(2)]
                b_S0b = [Buf(), Buf()]
                S0b += [tab[j][:, :].bitcast(BF16).rearrange("p (a b) -> p a b", a=2) for j in range(2)]
                b_S0b += [b_tab[0], b_tab[1]]
                qpad = T(st, "qpad", [128, 2, 1024], BF16)
                b_qpad = Buf()
                gt = [tmp[3], tmp[4]]
                b_gt = [b_tmp[3], b_tmp[4]]

                K.op("dve", lambda h: h.memset(qpad[:, :, :], 0.0), writes=[b_qpad])
                Sf = T(st, "Sf", [128, 2, 512], F32)
                b_Sf = [Buf(), Buf()]

                cnt = {"tmp": 0, "on": 0, "gt": 0, "s0": 0, "pt": 0, "s0b": 0}
                pendr = [None]
                def head_half(hd, hi, tok0, tiles):
                    if True:
                        nblk = 8 if hi == 0 else 9
                        for which in (1, 0):
                            w_, bw = wcols(wq_all, which * D + hd * 256)
                            rope_d = ropeq_d if which == 0 else ropek_d
                            for tt in tiles:
                                need_h(tt)
                                t0, N = TT[tt]
                                l0 = t0 - tok0
                                lb = [l0 // 128 + i for i in range((N + 127) // 128)]
                                for j in range(2):
                                    K.op("sp", lambda h, j=j, t0=t0, N=N, rope_d=rope_d, which=which: h.dma_start(
                                        out=tab[j][:, :N], in_=rope_d[hd, j, :, t0:t0 + N]),
                                        writes=[b_tab[j]], dma=True, name="tab")
                                ct, sn = tab[0], tab[1]
                                bct, bsn = b_tab[0], b_tab[1]
                                b1, bb1 = getbank()
                                mm_group(b1[:, :N], [(w_[:, kc, 0:128], hT[:, kc, t0:t0 + N]) for kc in range(8)],
                                         [bw] + [b_h[kc][tt] for kc in range(8)], bb1)
                                b2, bb2 = getbank()
                                mm_group(b2[:, :N], [(w_[:, kc, 128:256], hT[:, kc, t0:t0 + N]) for kc in range(8)],
                                         [bw] + [b_h[kc][tt] for kc in range(8)], bb2)
                                t2, ta, tb_ = tmp[0], tmp[1], tmp[2]
                                K.op("act", lambda h, b2=b2, N=N: h.copy(out=tmp[0][:, :N], in_=b2[:, :N]), reads=[bb2], writes=[b_tmp[0]])
                                K.op("dve", lambda h, b1=b1, N=N, ct=ct: h.tensor_tensor(out=tmp[1][:, :N], in0=b1[:, :N], in1=ct[:, :N], op=ALU.mult),
                                     reads=[bb1, bct], writes=[b_tmp[1]])
                                K.op("dve", lambda h, N=N, sn=sn: h.tensor_tensor(out=tmp[2][:, :N], in0=tmp[0][:, :N], in1=sn[:, :N], op=ALU.mult),
                                     reads=[b_tmp[0], bsn], writes=[b_tmp[2]])
                                K.op("dve", lambda h, N=N, l0=l0, which=which: h.tensor_tensor(out=QK[:, 2 * which, l0:l0 + N], in0=tmp[1][:, :N], in1=tmp[2][:, :N], op=ALU.subtract),
                                     reads=[b_tmp[1], b_tmp[2]], writes=[b_qk[2 * which][b] for b in lb])
                                K.op("dve", lambda h, b1=b1, N=N, sn=sn: h.tensor_tensor(out=tmp[3][:, :N], in0=b1[:, :N], in1=sn[:, :N], op=ALU.mult),
                                     reads=[bb1, bsn], writes=[b_tmp[3]])
                                K.op("dve", lambda h, N=N, ct=ct: h.tensor_tensor(out=tmp[4][:, :N], in0=tmp[0][:, :N], in1=ct[:, :N], op=ALU.mult),
                                     reads=[b_tmp[0], bct], writes=[b_tmp[4]])
                                K.op("dve", lambda h, N=N, l0=l0, which=which: h.tensor_tensor(out=QK[:, 2 * which + 1, l0:l0 + N], in0=tmp[3][:, :N], in1=tmp[4][:, :N], op=ALU.add),
                                     reads=[b_tmp[3], b_tmp[4]], writes=[b_qk[2 * which + 1][b] for b in lb])
                                bgq.tick()
                        wv = [wcols(wq_all, 2 * D + hd * 512 + j * 256) for j in range(2)]
                        for lbk in range(nblk):
                            rows = 64 if (hi == 1 and lbk == 8) else 128
                            g0 = tok0 + lbk * 128
                            tt = g0 // 512
                            bank, bb = getbank()
                            for j in range(2):
                                mm_group(bank[:rows, j * 256:(j + 1) * 256],
                                         [(hT[:, kc, g0:g0 + rows], wv[j][0][:, kc, :]) for kc in range(8)],
                                         [wv[j][1]] + [b_h[kc][tt] for kc in range(8)], bb, name="vproj")
                            K.op("act", lambda h, bank=bank, lbk=lbk, rows=rows: h.copy(out=vtok[:rows, lbk, :], in_=bank[:rows, :]),
                                 reads=[bb], writes=[b_vt[lbk]], name="vtok")
                            bgq.tick()
                        groups = [list(range(0, 4)), list(range(4, 8))] + ([[8]] if hi == 1 else [])
                        for grp in groups:
                            rows = 64 if grp == [8] else 128
                            ptbs["n"] += 1

                            def trk(h, grp=grp, rows=rows):
                                ins = None
                                for i, lbk in enumerate(grp):
                                    for kc in range(2):
                                        ins = h.transpose(out=ptb[:rows, i * 256 + kc * 128:i * 256 + (kc + 1) * 128],
                                                          in_=QK[:, 2 + kc, lbk * 128:lbk * 128 + rows], identity=identb[:, :])
                                return ins
                            K.op("pe", trk, reads=[b_qk[2][l] for l in grp] + [b_qk[3][l] for l in grp] + [b_const], writes=[b_ptb[0]], name="trk")
                            n = len(grp)
                            K.op("act", lambda h, grp=grp, rows=rows, n=n: h.copy(out=ktok[:rows, grp[0]:grp[0] + n, :],
                                                                                 in_=ptb[:rows, 0:n * 256].rearrange("p (b d) -> p b d", b=n)),
                                 reads=[b_ptb[0]], writes=[b_kt[l] for l in grp], name="ktok")
                        def chunk(lbk, rows=128, bo_=None):
                            c0 = lbk * 128
                            first = (hi == 0 and lbk == 0)
                            last = (hi == 1 and lbk == 7)
                            blk = slice(c0, c0 + rows)
                            pi = cnt["pt"] % 2
                            cnt["pt"] += 1
                            if bo_ is None:
                                bo, bbo = pbank[4 + pi], b_pbank[4 + pi]
                            else:
                                bo, bbo = bo_
                            kvb = []

                            def A1():
                                bs, bbs = getbank()
                                mm_group(bs[:rows, :rows], [(QK[:, 2 + kc, blk], QK[:, kc, blk]) for kc in range(2)],
                                         [b_qk[a_][lbk] for a_ in range(4)], bbs, name="scoresT")
                                K.op("dve", lambda h: h.tensor_tensor(out=PT[pi][:rows, :rows], in0=bs[:rows, :rows], in1=maskT[:rows, :rows], op=ALU.mult),
                                     reads=[bbs, b_const], writes=[b_PT[pi]], name="PT")

                            def A2():
                                pairs = [(PT[pi][:rows, :rows], vtok[:rows, lbk, :])]
                                rd = [b_PT[pi], b_vt[lbk]]
                                if not first:
                                    pairs += [(QK[:, kc, blk], Sb[:, kc, :]) for kc in range(2)]
                                    rd += [b_Sb, b_qk[0][lbk], b_qk[1][lbk]]
                                mm_group(bo[:rows, :], pairs, rd, bbo, name="o")

                            def AK():
                                for kc in range(2):
                                    bk, bbk = getbank()
                                    kvb.append((bk, bbk))
                                    K.op("pe", lambda h, bk=bk, kc=kc: h.matmul(
                                        bk[:, :], lhsT=ktok[:, lbk, kc * 128:(kc + 1) * 128], rhs=vtok[:, lbk, :], start=True, stop=True),
                                        reads=[b_kt[lbk], b_vt[lbk]], writes=[bbk], name="kv")

                            def A3():
                                for kc in range(2):
                                    bk, bbk = kvb[kc]
                                    if first:
                                        K.op("dve", lambda h, bk=bk, kc=kc: h.tensor_copy(out=Sf[:, kc, :], in_=bk[:, :]),
                                             reads=[bbk], writes=[b_Sf[kc]], name="Sinit")
                                    else:
                                        K.op("dve", lambda h, bk=bk, kc=kc: h.tensor_tensor(out=Sf[:, kc, :], in0=Sf[:, kc, :], in1=bk[:, :], op=ALU.add),
                                             reads=[bbk, b_Sf[kc]], writes=[b_Sf[kc]], name="Sacc")

                            def A4():
                                if not last:
                                    K.op("act", lambda h: h.copy(out=Sb[:, :, :], in_=Sf[:, :, :]), reads=[b_Sf[0], b_Sf[1]], writes=[b_Sb], name="Sb")
                                else:
                                    for kc in range(2):
                                        K.op("act", lambda h, kc=kc: h.activation(out=Sf[:, kc, :], in_=Sf[:, kc, :], func=AF.Copy, scale=g2048[hd]),
                                             reads=[b_Sf[kc]], writes=[b_Sf[kc]], name="retp")
                                    K.op("sp", lambda h: h.dma_start(out=retp_d[hd].rearrange("(kc p) v -> p kc v", p=128), in_=Sf[:, :, :]),
                                         reads=[b_Sf[0], b_Sf[1]], dma=True, name="retp_out")

                            oi = cnt["on"] % 2
                            cnt["on"] += 1
                            st_, mv_, bst_, bmv_ = stats[oi], mv[oi], b_stats[oi], b_mv[oi]

                            def B1():
                                K.op("dve", lambda h: h.bn_stats(out=st_[:rows, 0:6], in_=bo[:rows, :]), reads=[bbo], writes=[bst_], name="bnst")
                                K.op("dve", lambda h: h.bn_aggr(out=mv_[:rows, 0:2], in_=st_[:rows, 0:6]), reads=[bst_, bmv_], writes=[bmv_], name="bnag")
                                K.op("act", lambda h: h.activation(out=mv_[:rows, 2:3], in_=mv_[:rows, 1:2], func=AF.Sqrt, bias=epsc[:rows, 1:2], scale=1.0),
                                     reads=[bmv_, b_const], writes=[bmv_], name="gnsqrt")

                            def B2():
                                K.op("dve", lambda h: h.reciprocal(out=mv_[:rows, 3:4], in_=mv_[:rows, 2:3]), reads=[bmv_], writes=[bmv_], name="gnrec")
                                K.op("dve", lambda h: h.tensor_scalar(out=mv_[:rows, 4:5], in0=mv_[:rows, 0:1], scalar1=mv_[:rows, 3:4], scalar2=-1.0,
                                                                      op0=ALU.mult, op1=ALU.mult),
                                     reads=[bmv_], writes=[bmv_], name="gnnb")
                                K.op("act", lambda h: h.activation(out=on[oi][:rows, :], in_=bo[:rows, :], func=AF.Identity, bias=mv_[:rows, 4:5], scale=mv_[:rows, 3:4]),
                                     reads=[bbo, bmv_], writes=[b_on[oi]], name="gnorm")

                            ph = ptbs["n"] % 2
                            ptbs["n"] += 1

                            def B3():
                                def tro(h):
                                    ins = None
                                    for m in range(4):
                                        ins = h.transpose(out=ptb[:, ph * 512 + m * 128:ph * 512 + m * 128 + rows], in_=on[oi][:rows, m * 128:(m + 1) * 128],
                                                          identity=identb[:rows, :rows])
                                    return ins
                                K.op("pe", tro, reads=[b_on[oi], b_const], writes=[b_ptb[ph]], name="tro")

                            def B4():
                                K.op("act", lambda h: h.copy(out=QK[:, :, c0:c0 + rows],
                                                             in_=ptb[:, ph * 512:ph * 512 + 512].rearrange("p (m t) -> p m t", m=4)[:, :, :rows]),
                                     reads=[b_ptb[ph]], writes=[b_qk[a_][lbk] for a_ in range(4)], name="oT")
                            return dict(A1=A1, AK=AK, A2=A2, A3=A3, A4=A4, B1=B1, B2=B2, B3=B3, B4=B4)

                        def state_load(s, par):
                            si = s % NS0
                            km, bkm = kmk[s % 2], b_kmk[s % 2]
                            K.op("sp", lambda h: h.dma_start(out=sS0f[si][:, :, :], in_=sret[s, hd].rearrange("(kc p) v -> p kc v", p=128)),
                                 writes=[b_sS0f[si]], dma=True, name="s0f")
                            K.op("dve", lambda h: h.tensor_scalar(out=km[:64, :], in0=sk[par][:64, :], scalar1=kmcol[:, hd * 16 + s:hd * 16 + s + 1],
                                                                  scalar2=None, op0=ALU.mult),
                                 reads=[b_skv[par], b_const], writes=[bkm], name="kmask")

                        def state_comp(s, par):
                            si = s % NS0
                            km, bkm = kmk[s % 2], b_kmk[s % 2]
                            for kc in range(2):
                                bk, bbk = getbank()
                                K.op("pe", lambda h, bk=bk, kc=kc: h.matmul(bk[:, :], lhsT=km[:64, kc * 128:(kc + 1) * 128], rhs=sv[par][:64, :], start=True, stop=True),
                                     reads=[bkm, b_skv[par]], writes=[bbk], name="kv_s")
                                K.op("dve", lambda h, bk=bk, kc=kc: h.scalar_tensor_tensor(
                                    out=sS0f[si][:, kc, :], in0=sS0f[si][:, kc, :], scalar=g4[hd], in1=bk[:, :], op0=ALU.mult, op1=ALU.add),
                                    reads=[bbk, b_sS0f[si]], writes=[b_sS0f[si]], name="snew")
                            K.op("sp", lambda h: h.dma_start(out=rets_d[s, hd].rearrange("(kc p) v -> p kc v", p=128), in_=sS0f[si][:, :, :]),
                                 reads=[b_sS0f[si]], dma=True, name="rets_out")

                        def s0b_load(s):
                            si = s % 4
                            K.op("pool", lambda h: h.dma_start(out=S0b[si][:, :, :], in_=sret[s, hd].rearrange("(kc p) v -> p kc v", p=128)),
                                 writes=[b_S0b[si]], dma=True, name="s0b")

                        def cross_seq(s, bo6):
                            si = s % 4

                            def cross(h):
                                ins = None
                                for kc in range(2):
                                    ins = h.matmul(bo6[:64, :], lhsT=qpad[:, kc, s * 64:(s + 1) * 64], rhs=S0b[si][:, kc, :], start=False, stop=(s == 15 and kc == 1))
                                return ins
                            K.op("pe", cross, reads=[b_qpad, b_S0b[si]], writes=[b_pbank[6]], name="cross_s")

                        pstate["avail"] = [0, 1, 2, 3]
                        if hi == 1:
                            par = hd % 2
                            bo6 = pbank[6]
                            sblk = slice(1024, 1088)
                            bs, bbs = getbank()
                            mm_group(bs[:64, :64], [(QK[:, 2 + kc, sblk], QK[:, kc, sblk]) for kc in range(2)],
                                     [b_qk[a_][8] for a_ in range(4)], bbs, name="scoresT_s")
                            K.op("dve", lambda h: h.tensor_tensor(out=PTs[:64, :64], in0=bs[:64, :64], in1=smask[:64, :64], op=ALU.mult),
                                 reads=[bbs, b_const], writes=[b_PTs], name="PTs")
                            K.op("pe", lambda h: h.matmul(bo6[:64, :], lhsT=PTs[:64, :64], rhs=vtok[:64, 8, :], start=True, stop=False),
                                 reads=[b_PTs, b_vt[8]], writes=[b_pbank[6]], name="o_s_inner")
                            for kc in range(2):
                                K.op("dve", lambda h, kc=kc: h.tensor_copy(
                                    out=qpad[:, kc, 0:1020].rearrange("p (s e) -> p s e", e=68)[:, :, 0:4],
                                    in_=QK[:, kc, 1024:1084].rearrange("p (s t) -> p s t", t=4)),
                                    reads=[b_qk[kc][8], b_qpad], writes=[b_qpad], name="qpad")
                                K.op("dve", lambda h, kc=kc: h.tensor_copy(out=qpad[:, kc, 1020:1024], in_=QK[:, kc, 1084:1088]),
                                     reads=[b_qk[kc][8], b_qpad], writes=[b_qpad], name="qpad2")
                            K.op("act", lambda h: h.copy(out=sk[par][:64, :], in_=ktok[:64, 8, :]), reads=[b_kt[8]], writes=[b_skv[par]], name="skcp")
                            K.op("act", lambda h: h.copy(out=sv[par][:64, :], in_=vtok[:64, 8, :]), reads=[b_vt[8], b_skv[par]], writes=[b_skv[par]], name="svcp")
                            bgq.add(lambda: state_load(0, par))
                            for s_ in range(16):
                                if s_ + 1 < 16:
                                    bgq.add(lambda s_=s_: state_load(s_ + 1, par))
                                bgq.add(lambda s_=s_: state_comp(s_, par))
                        chs = [chunk(lbk) for lbk in range(8)]
                        nop = lambda: None
                        chs[0]["A1"](); chs[0]["AK"](); chs[0]["A2"](); chs[0]["A3"](); chs[0]["A4"]()
                        if hi == 1:
                            for s_ in range(4):
                                s0b_load(s_)
                        for ci in range(8):
                            cur = chs[ci]
                            nxt = chs[ci + 1] if ci + 1 < 8 else dict(A1=nop, AK=nop, A2=nop, A3=nop, A4=nop)
                            cur["B1"]()
                            nxt["A1"]()
                            nxt["AK"]()
                            nxt["A2"]()
                            cur["B2"]()
                            nxt["A3"]()
                            nxt["A4"]()
                            cur["B3"]()
                            cur["B4"]()
                            if hi == 1:
                                cross_seq(2 * ci, bo6)
                                cross_seq(2 * ci + 1, bo6)
                                if ci < 6:
                                    s0b_load(2 * ci + 4)
                                    s0b_load(2 * ci + 5)
                            bgq.tick()
                        if hi == 1:
                            sc = chunk(8, rows=64, bo_=(bo6, b_pbank[6]))
                            sc["B1"](); sc["B2"](); sc["B3"](); sc["B4"]()
                        pstate["avail"] = [0, 1, 2, 3, 4, 5, 6]

                        wg = [wcols(wq_all, 4 * D + hd * 512 + j * 256) for j in range(2)]
                        for m in range(4):
                            w_, bw = wg[m // 2]
                            sl = slice((m % 2) * 128, (m % 2) * 128 + 128)
                            for tt in tiles:
                                t0, N = TT[tt]
                                l0 = t0 - tok0
                                lb = [l0 // 128 + i for i in range((N + 127) // 128)]
                                bank, bb = getbank()
                                mm_group(bank[:, :N], [(w_[:, kc, sl], hT[:, kc, t0:t0 + N]) for kc in range(8)],
                                         [bw] + [b_h[kc][tt] for kc in range(8)], bb, name="gate")
                                gi = cnt["gt"] % 2
                                cnt["gt"] += 1
                                K.op("act", lambda h, bank=bank, N=N, gi=gi: h.activation(out=gt[gi][:, :N], in_=bank[:, :N], func=AF.Silu),
                                     reads=[bb], writes=[b_gt[gi]], name="gsilu")
                                K.op("dve", lambda h, m=m, l0=l0, N=N, gi=gi: h.scalar_tensor_tensor(
                                    out=QK[:, m, l0:l0 + N], in0=QK[:, m, l0:l0 + N], scalar=gncol(hd * 4 + m), in1=gt[gi][:, :N], op0=ALU.mult, op1=ALU.mult),
                                    reads=[b_gt[gi], b_const] + [b_qk[m][b] for b in lb], writes=[b_qk[m][b] for b in lb], name="gating")
                                bgq.tick()
                        wo = [wrows(ret_w_out, hd * 512 + j * 256) for j in range(2)]
                        for tt in tiles:
                            for mo in range(8):
                                t0, N = TT[tt]
                                l0 = t0 - tok0
                                lb = [l0 // 128 + i for i in range((N + 127) // 128)]
                                bank, bb = getbank()
                                pairs = [(wo[m // 2][0][:, m % 2, mo * 128:(mo + 1) * 128], QK[:, m, l0:l0 + N]) for m in range(4)]
                                mm_group(bank[:, :N], pairs, [w[1] for w in wo] + [b_qk[m][b] for m in range(4) for b in lb], bb, name="wout")
                                residual_add(bank, bb, mo, tt, t0, N)
                                bgq.tick()
                            if hd == NH - 1:
                                if pendr[0] is not None:
                                    pendr[0]()
                                pendr[0] = norm_tile(3, tt, defer=True)
                        if hd == NH - 1:
                            if hi == 1:
                                pending_h[4] = pendr[0]
                            else:
                                pendr[0]()
                            pendr[0] = None
                for hd_ in range(NH):
                    for hi_, (tok0_, tiles_) in enumerate(halves):
                        head_half(hd_, hi_, tok0_, tiles_)
                pstate["avail"] = [0, 1, 2, 3, 4, 5, 6]
                pstate["next"] = 0
                K.fence()

        def make_final_cb(st):
            yT = T(st, "yT", [128, 8, 256], F32)
            b_y = [Buf() for _ in range(8)]
            yst = [T(st, f"yst{i}", [128, D], F32) for i in range(2)]
            b_yst = [Buf(), Buf()]
            state = {"n": 0}

            def cb(tt, t0, N, rs_, brs_):
                for h0 in range(0, N, 256):
                    n = min(256, N - h0)
                    for c in range(8):
                        K.op("dve", lambda h, c=c, h0=h0, n=n: h.scalar_tensor_tensor(out=yT[:, c, :n], in0=xT[:, c, t0 + h0:t0 + h0 + n], scalar=gcol(4, c),
                                                                                      in1=rs_[:, h0:h0 + n], op0=ALU.mult, op1=ALU.mult),
                             reads=[b_x[c][tt], brs_, b_const], writes=[b_y[c]], name="ynorm")
                    for i in range((n + 127) // 128):
                        rows = min(128, n - i * 128)
                        tb = (t0 + h0) // 128 + i
                        s_ = state["n"] % 2
                        state["n"] += 1
                        for half in range(2):
                            bank, bb = getbank()

                            def tr(h, bank=bank, i=i, rows=rows, half=half):
                                ins = None
                                for q in range(4):
                                    cc = half * 4 + q
                                    ins = h.transpose(out=bank[:rows, q * 128:(q + 1) * 128], in_=yT[:, cc, i * 128:i * 128 + rows], identity=ident[:, :])
                                return ins
                            K.op("pe", tr, reads=[b_y[half * 4 + q] for q in range(4)] + [b_const], writes=[bb], name="ytr")
                            eng = "act" if half == 0 else "dve"

                            def ev(h, bank=bank, s_=s_, rows=rows, half=half, eng=eng):
                                if eng == "act":
                                    return h.copy(out=yst[s_][:rows, half * 512:(half + 1) * 512], in_=bank[:rows, :])
                                return h.tensor_copy(out=yst[s_][:rows, half * 512:(half + 1) * 512], in_=bank[:rows, :])
                            K.op(eng, ev, reads=[bb], writes=[b_yst[s_]], name="yev")
                        K.op("sp", lambda h, s_=s_, tb=tb, rows=rows: h.dma_start(out=y_d[tb * 128:tb * 128 + rows, :], in_=yst[s_][:rows, :]),
                             reads=[b_yst[s_]], dma=True, name="yout")
            return cb

        def final_stage():
            with contextlib.ExitStack() as st:
                cb = make_final_cb(st)
                for tt in range(5):
                    norm_tile(4, tt, final_cb=cb)
                K.barrier()

        stages = [("conv", conv_stage), ("ffn0", lambda: ffn_stage(0)), ("ret", ret_stage), ("ffn1", lambda: ffn_stage(1))]
        if stop_after == "load":
            stages = []
        for nm, fnc in stages:
            fnc()
            if stop_after == nm:
                break
        if stop_after is not None:
            final_stage()
        K.emit()
    return nc, K


_CACHE = {}


def kernel(x_prompt, x_sample, state_conv, state_ret, norm_mix, norm_ffn, conv_w_in, conv_w,
           conv_w_out, ret_w_in, ret_gn, ret_w_out, ffn_w_gate, ffn_w_up, ffn_w_down, final_norm,
           _stop_after=None):
    f = lambda a: np.ascontiguousarray(np.asarray(a, dtype=np.float32))
    x_prompt, x_sample, state_conv, state_ret = f(x_prompt), f(x_sample), f(state_conv), f(state_ret)
    consts, g4, g2048 = _consts()
    key = ("nc", _stop_after)
    if key not in _CACHE:
        _CACHE[key] = build_nc(g4, g2048, stop_after=_stop_after)
    nc, _ = _CACHE[key]
    def colify(v):
        return f(v).reshape(-1, 128).T
    cols = np.concatenate([colify(norm_mix[0]), colify(norm_ffn[0]), colify(norm_mix[1]), colify(norm_ffn[1]),
                           colify(final_norm), colify(f(conv_w)[0, 0]), colify(f(conv_w)[0, 1]), colify(f(conv_w)[0, 2]),
                           colify(f(ret_gn)[0])], axis=1)
    cols = np.ascontiguousarray(cols, dtype=np.float32)
    shared = dict(cols=cols, conv_w_in=f(conv_w_in)[0], conv_w_out=f(conv_w_out)[0], ret_w_in=f(ret_w_in)[0],
                  ret_w_out=f(ret_w_out)[0], ffn_w_gate=f(ffn_w_gate), ffn_w_up=f(ffn_w_up), ffn_w_down=f(ffn_w_down), **consts)
    in_maps = []
    for c in range(8):
        m = dict(shared)
        m["xin"] = np.ascontiguousarray(np.concatenate([x_prompt[c], x_sample[16 * c:16 * c + 16].reshape(64, D)], axis=0))
        m["sconv"] = np.ascontiguousarray(state_conv[0, 16 * c:16 * c + 16].reshape(32, D))
        m["sret"] = np.ascontiguousarray(state_ret[0, 16 * c:16 * c + 16])
        in_maps.append(m)
    res = run_bass_kernel_spmd(nc, in_maps, core_ids=list(range(8)))
    R = res.results
    y_prompt = np.stack([R[c]["y"][:2048] for c in range(8)], axis=0)
    y_sample = np.concatenate([R[c]["y"][2048:].reshape(16, 4, D) for c in range(8)], axis=0)
    conv_prompt = np.stack([R[c]["convo"][0:2] for c in range(8)], axis=0)[None]
    conv_sample = np.concatenate([R[c]["convo"][2:34].reshape(16, 2, D) for c in range(8)], axis=0)[None]
    ret_prompt = np.stack([R[c]["retp"] for c in range(8)], axis=0)[None]
    ret_sample = np.concatenate([R[c]["rets"] for c in range(8)], axis=0)[None]
    return (y_prompt.astype(np.float32), y_sample.astype(np.float32), conv_prompt.astype(np.float32),
            conv_sample.astype(np.float32), ret_prompt.astype(np.float32), ret_sample.astype(np.float32))
```

```python
import contextlib
import numpy as np
import concourse.bass as bass
import concourse.mybir as mybir
from concourse.bass_utils import run_bass_kernel_spmd

F32 = mybir.dt.float32
BF16 = mybir.dt.bfloat16
AF = mybir.ActivationFunctionType
ALU = mybir.AluOpType

D = 1024
NT = 2112
TT = [(0, 512), (512, 512), (1024, 512), (1536, 512), (2048, 64)]
DFF = 2816
NH = 4
RMS_EPS = 1e-6
GN_EPS = 1e-6
ENGS = ("pe", "act", "dve", "pool", "sp")
NSLOT = 6


class Buf:
    __slots__ = ("name", "w", "r", "fence")
    FENCE = None

    def __init__(self, name=""):
        self.name = name
        self.w = None
        self.r = []
        self.fence = Buf.FENCE


class Op:
    __slots__ = ("eng", "fn", "deps", "dma", "sig", "sem", "val", "pre", "name")


class Kern:
    def __init__(self, nc, n_dma_sems=16):
        self.nc = nc
        self.ops = []
        self.n_dma_sems = n_dma_sems
        self.last = {e: None for e in ENGS}
        self.dmas = []

    def op(self, eng, fn, reads=(), writes=(), dma=False, name="", extra=()):
        o = Op()
        o.eng, o.fn, o.dma, o.name = eng, fn, dma, name
        o.sig, o.sem, o.val, o.pre = False, None, 0, None
        deps = set(extra)
        for b in reads:
            if b.w is not None:
                deps.add(b.w)
            if b.fence is not None:
                deps.update(b.fence)
        for b in writes:
            if b.w is not None:
                deps.add(b.w)
            deps.update(b.r)
            if b.fence is not None:
                deps.update(b.fence)
                b.fence = None
        if eng == "pe":
            deps = {d for d in deps if not (d.eng == "pe" and not d.dma)}
        o.deps = deps
        for b in reads:
            b.r.append(o)
        for b in writes:
            b.w = o
            b.r = []
        self.ops.append(o)
        if dma:
            self.dmas.append(o)
        else:
            self.last[eng] = o
        return o

    def fence(self):
        fr = [o for o in self.last.values() if o is not None] + list(self.dmas)
        Buf.FENCE = fr

    def barrier(self):
        ex = [o for o in self.last.values() if o is not None] + list(self.dmas)
        self.dmas = []
        saved = dict(self.last)
        for e in ENGS:
            self.op(e, lambda h: None, extra=ex, name="barrier")
        self.last = saved

    def emit(self):
        nc = self.nc
        ops = self.ops
        for o in ops:
            if o.dma:
                o.sig = True
            for d in o.deps:
                d.sig = True
        with contextlib.ExitStack() as st:
            esem = {e: st.enter_context(nc.semaphore(f"s_{e}")) for e in ENGS if e != "sp"}
            nd = self.n_dma_sems
            dsem = {e: [st.enter_context(nc.semaphore(f"s_dma_{e}{i}")) for i in range(nd)] for e in ("sp", "pool", "act")}
            cnt = {e: 0 for e in ENGS}
            ndma = {e: 0 for e in ENGS}
            for o in ops:
                if o.dma:
                    k = ndma[o.eng]
                    ndma[o.eng] += 1
                    o.sem = dsem[o.eng][k % nd]
                    o.val = 16 * (k // nd + 1)
                    o.pre = (o.sem, o.val - 16) if o.val > 16 else None
                elif o.sig:
                    assert o.eng != "sp"
                    cnt[o.eng] += 1
                    o.sem = esem[o.eng]
                    o.val = cnt[o.eng]
            streams = {e: [o for o in ops if o.eng == e] for e in ENGS}
            self.stats = {e: len(streams[e]) for e in ENGS}
            self.stats["sig"] = dict(cnt)
            block = st.enter_context(nc.Block())

            def run(e, h):
                known = {}
                for o in streams[e]:
                    need = {}
                    for d in o.deps:
                        k = id(d.sem)
                        if need.get(k, (None, 0))[1] < d.val:
                            need[k] = (d.sem, d.val)
                    if o.pre is not None:
                        k = id(o.pre[0])
                        if need.get(k, (None, 0))[1] < o.pre[1]:
                            need[k] = o.pre
                    for k, (s, v) in need.items():
                        if known.get(k, 0) < v:
                            h.wait_ge(s, v)
                            known[k] = v
                    ins = o.fn(h)
                    if o.sig:
                        assert ins is not None, o.name
                        ins.then_inc(o.sem, 16 if o.dma else 1)

            @block.tensor
            def _(h):
                run("pe", h)

            @block.scalar
            def _(h):
                run("act", h)

            @block.vector
            def _(h):
                run("dve", h)

            @block.gpsimd
            def _(h):
                run("pool", h)

            @block.sync
            def _(h):
                run("sp", h)


def _consts():
    half = 128
    inv = 10000.0 ** (-(np.arange(half, dtype=np.float64) / half))
    pos = np.concatenate([np.arange(2048, dtype=np.float64), np.tile(16384.0 + np.arange(4, dtype=np.float64), 16)])
    ang = pos[None, :] * inv[:, None]
    cos = np.cos(ang)
    sin = np.sin(ang)
    e = np.concatenate([np.arange(2048, dtype=np.float64), np.tile(np.arange(4, dtype=np.float64), 16)]) + 1.0
    g = np.array([1.0 - 2.0 ** (-5.0 - h) for h in range(NH)], dtype=np.float64)
    lg = np.log(g.astype(np.float32)).astype(np.float64)
    ropeq = np.zeros((NH, 2, 128, NT), np.float32)
    ropek = np.zeros((NH, 2, 128, NT), np.float32)
    for h in range(NH):
        sq = np.exp(e * lg[h])[None, :]
        sk = np.exp(-e * lg[h])[None, :] * (256.0 ** -0.5)
        ropeq[h, 0] = cos * sq
        ropeq[h, 1] = sin * sq
        ropek[h, 0] = cos * sk
        ropek[h, 1] = sin * sk
    j = np.arange(128)
    maskT = (j[:, None] <= j[None, :]).astype(np.float32)
    js = np.arange(64)
    smask = ((js[:, None] // 4 == js[None, :] // 4) & (js[:, None] <= js[None, :])).astype(np.float32)
    kmcol = np.zeros((64, NH * 16), np.float32)
    for h in range(NH):
        for s in range(16):
            kmcol[4 * s:4 * s + 4, h * 16 + s] = np.exp(4.0 * lg[h])
    g4 = [float(np.exp(4.0 * lg[h])) for h in range(NH)]
    g2048 = [float(np.exp(2048.0 * lg[h])) for h in range(NH)]
    return dict(ropeq=ropeq, ropek=ropek, maskT=maskT, smask=smask, kmcol=kmcol,
                ident=np.eye(128, dtype=np.float32), ones=np.ones((128, 128), np.float32)), g4, g2048


def build_nc(g4, g2048, stop_after=None):
    nc = bass.Bass("TRN2", target_bir_lowering=False)

    def din(name, shape):
        return nc.dram_tensor(name, list(shape), F32, kind="ExternalInput").ap()

    def dout(name, shape):
        return nc.dram_tensor(name, list(shape), F32, kind="ExternalOutput").ap()

    xin = din("xin", [NT, D])
    sconv = din("sconv", [32, D])
    sret = din("sret", [16, NH, 256, 512])
    cols_d = din("cols", [128, 80])
    conv_w_in = din("conv_w_in", [D, 3 * D])
    conv_w_out = din("conv_w_out", [D, D])
    ret_w_in = din("ret_w_in", [D, 6 * D])
    ret_w_out = din("ret_w_out", [2 * D, D])
    ffn_w_gate = din("ffn_w_gate", [2, D, DFF])
    ffn_w_up = din("ffn_w_up", [2, D, DFF])
    ffn_w_down = din("ffn_w_down", [2, DFF, D])
    ropeq_d = din("ropeq", [NH, 2, 128, NT])
    ropek_d = din("ropek", [NH, 2, 128, NT])
    maskT_d = din("maskT", [128, 128])
    smask_d = din("smask", [64, 64])
    kmcol_d = din("kmcol", [64, NH * 16])
    ident_d = din("ident", [128, 128])
    ones_d = din("ones", [128, 128])
    y_d = dout("y", [NT, D])
    convo_d = dout("convo", [34, D])
    retp_d = dout("retp", [NH, 256, 512])
    rets_d = dout("rets", [16, NH, 256, 512])

    Buf.FENCE = None
    K = Kern(nc)
    with contextlib.ExitStack() as top:
        uniq = {"n": 0}

        def T(st, name, shape, dt):
            uniq["n"] += 1
            return st.enter_context(nc.sbuf_tensor(f"sb{uniq['n']}_{name}", list(shape), dt))

        def PS(name, shape, dt):
            return top.enter_context(nc.psum_tensor("ps_" + name, list(shape), dt))

        xT = T(top, "xT", [128, 8, NT], F32)
        hT = T(top, "hT", [128, 8, NT], BF16)
        ring = [T(top, f"ring{i}", [128, 2048], BF16) for i in range(NSLOT)]
        ident = T(top, "ident", [128, 128], F32)
        identb = T(top, "identb", [128, 128], BF16)
        ones = T(top, "ones", [128, 128], F32)
        cols = T(top, "colsb", [128, 80], F32)
        maskT = T(top, "maskT", [128, 128], F32)
        smask = T(top, "smask", [64, 64], F32)
        kmcol = T(top, "kmcol", [64, NH * 16], F32)
        acc2 = [T(top, f"acc{i}", [128, 512], F32) for i in range(2)]
        sq = [T(top, f"sq{i}", [128, 512], F32) for i in range(2)]
        rs2 = [T(top, f"rs{i}", [128, 512], F32) for i in range(2)]

        NS0 = 2
        sS0f = [T(top, f"sS0f{i}", [128, 2, 512], F32) for i in range(NS0)]
        b_sS0f = [Buf() for _ in range(NS0)]
        sk = [T(top, f"sk{i}", [64, 256], BF16) for i in range(2)]
        sv = [T(top, f"sv{i}", [64, 512], BF16) for i in range(2)]
        b_skv = [Buf(), Buf()]
        kmk = [T(top, f"kmk{i}", [64, 256], BF16) for i in range(2)]
        b_kmk = [Buf(), Buf()]

        class BG:
            def __init__(self):
                self.q = []
                self.n = 0
                self.stride = 3

            def add(self, fn):
                self.q.append(fn)

            def tick(self):
                self.n += 1
                if self.q and self.n % self.stride == 0:
                    self.q.pop(0)()

            def drain(self):
                while self.q:
                    self.q.pop(0)()
        bgq = BG()

        b_x = [[Buf(f"x{c}_{t}") for t in range(5)] for c in range(8)]
        b_h = [[Buf(f"h{c}_{t}") for t in range(5)] for c in range(8)]
        b_ring = [Buf(f"ring{i}") for i in range(NSLOT)]
        b_const = Buf("const")
        b_acc2, b_rs2 = [Buf(), Buf()], [Buf(), Buf()]
        b_sq = [Buf("sq0"), Buf("sq1")]

        pbank = [PS(f"pb{i}", [128, 512], F32) for i in range(7)]
        b_pbank = [Buf(f"pb{i}") for i in range(7)]
        ptb = PS("ptb", [128, 1024], BF16)
        _bp = Buf("ptb")
        b_ptb = [_bp, _bp]
        ptbs = {"n": 0}
        pstate = {"next": 0, "avail": [0, 1, 2, 3, 4, 5, 6]}

        def getbank():
            a = pstate["avail"]
            i = a[pstate["next"] % len(a)]
            pstate["next"] += 1
            return pbank[i], b_pbank[i]

        ring_state = {"next": 0}
        x_ops = []

        def wload(src_ap, shape3):
            i = ring_state["next"] % NSLOT
            ring_state["next"] += 1
            a, b = shape3
            view = ring[i][:, 0:a * b].rearrange("p (a b) -> p a b", a=a)
            ex = ()
            if ring_state["next"] == 1:
                ex = tuple(x_ops)
            K.op("pool", lambda h: h.dma_start(out=view, in_=src_ap), writes=[b_ring[i]], dma=True, name="wload", extra=ex)
            return view, b_ring[i]

        def wcols(w2d, col0, ncol=256):
            src = w2d.rearrange("(kc p) n -> p kc n", p=128)[:, :, col0:col0 + ncol]
            return wload(src, (8, ncol))

        def wrows(w2d, row0):
            src = w2d[row0:row0 + 256, :].rearrange("(r p) n -> p r n", p=128)
            return wload(src, (2, 1024))

        for (dst, src, nm) in ((ident, ident_d, "ident"), (ones, ones_d, "ones"), (cols, cols_d, "cols"),
                               (maskT, maskT_d, "maskT"), (smask, smask_d, "smask"), (kmcol, kmcol_d, "kmcol")):
            K.op("sp" if nm == "ident" else "act", lambda h, dst=dst, src=src: h.dma_start(out=dst[:], in_=src),
                 writes=[b_const], dma=True, name=nm)
        K.op("dve", lambda h: h.tensor_copy(out=identb[:], in_=ident[:]), reads=[b_const], writes=[b_const])
        EPS_R = None

        def gcol(gi, c):
            return cols[:, gi * 8 + c:gi * 8 + c + 1]

        def wccol(j, c):
            return cols[:, 40 + j * 8 + c:40 + j * 8 + c + 1]

        def gncol(m):
            return cols[:, 64 + m:64 + m + 1]

        epsc = T(top, "epsc", [128, 2], F32)
        K.op("dve", lambda h: h.memset(epsc[:, 0:1], RMS_EPS), writes=[b_const])
        K.op("dve", lambda h: h.memset(epsc[:, 1:2], GN_EPS), reads=[b_const], writes=[b_const])

        def mm_group(out_ap, pairs, reads, bbuf, name="mm"):
            n = len(pairs)

            def fn(h):
                ins = None
                for i, (l, r) in enumerate(pairs):
                    ins = h.matmul(out_ap, lhsT=l, rhs=r, start=(i == 0), stop=(i == n - 1))
                return ins
            return K.op("pe", fn, reads=reads, writes=[bbuf], name=name)

        nstate = {"n": 0}
        pending_h = {}

        def need_h(tt):
            f = pending_h.pop(tt, None)
            if f is not None:
                f()


        def norm_tile(gi, tt, final_cb=None, defer=False):
            t0, N = TT[tt]
            p = nstate["n"] % 2
            nstate["n"] += 1
            acc_, rs_, bacc_, brs_ = acc2[p], rs2[p], b_acc2[p], b_rs2[p]
            for c in range(8):
                if c == 0:
                    K.op("act", lambda h: h.activation(out=acc_[:, :N], in_=xT[:, 0, t0:t0 + N], func=AF.Square),
                         reads=[b_x[0][tt]], writes=[bacc_], name="sq0")
                else:
                    s_ = c % 2
                    K.op("act", lambda h, c=c, s_=s_: h.activation(out=sq[s_][:, :N], in_=xT[:, c, t0:t0 + N], func=AF.Square),
                         reads=[b_x[c][tt]], writes=[b_sq[s_]], name="sq")
                    K.op("dve", lambda h, s_=s_: h.tensor_tensor(out=acc_[:, :N], in0=acc_[:, :N], in1=sq[s_][:, :N], op=ALU.add),
                         reads=[b_sq[s_], bacc_], writes=[bacc_], name="sqadd")
            def part2():
                norm_part2(gi, tt, t0, N, acc_, rs_, bacc_, brs_, final_cb)
            if defer:
                return part2
            part2()

        def norm_part2(gi, tt, t0, N, acc_, rs_, bacc_, brs_, final_cb):
            bank, bb = getbank()
            K.op("pe", lambda h: h.matmul(bank[:, :N], lhsT=ones[:, :], rhs=acc_[:, :N], start=True, stop=True),
                 reads=[bacc_, b_const], writes=[bb], name="onesmm")
            K.op("act", lambda h: h.activation(out=rs_[:, :N], in_=bank[:, :N], func=AF.Ln, bias=epsc[:, 0:1], scale=1.0 / D),
                 reads=[bb, b_const], writes=[brs_], name="rln")
            K.op("act", lambda h: h.activation(out=rs_[:, :N], in_=rs_[:, :N], func=AF.Exp, scale=-0.5),
                 reads=[brs_], writes=[brs_], name="rexp")
            if final_cb is not None:
                final_cb(tt, t0, N, rs_, brs_)
                return
            for c in range(8):
                K.op("dve", lambda h, c=c: h.scalar_tensor_tensor(
                    out=hT[:, c, t0:t0 + N], in0=xT[:, c, t0:t0 + N], scalar=gcol(gi, c), in1=rs_[:, :N],
                    op0=ALU.mult, op1=ALU.mult),
                    reads=[b_x[c][tt], brs_, b_const], writes=[b_h[c][tt]], name="hnorm")

        def residual_add(bank, bb, mo, tt, t0, N, eng="dve"):
            K.op(eng, lambda h: h.tensor_tensor(out=xT[:, mo, t0:t0 + N], in0=xT[:, mo, t0:t0 + N], in1=bank[:, :N], op=ALU.add),
                 reads=[bb, b_x[mo][tt]], writes=[b_x[mo][tt]], name="resadd")

        with contextlib.ExitStack() as st:
            NX = 3
            xst2 = [T(st, f"xst{i}", [128, 2, D], F32) for i in range(NX)]
            b_xst2 = [Buf(f"xst{i}") for i in range(NX)]
            pend0 = [None]
            junk = T(st, "junk", [128, D], F32)
            b_junk = Buf("junk")
            ss = [T(st, f"ss{i}", [128, 4], F32) for i in range(2)]
            b_ss = [Buf("ss0"), Buf("ss1")]

            def norm0_block(tt, tb, rows, xs_, b_xs):
                j, p = tb % 4, tt % 2
                if rows < 128:
                    K.op("dve", lambda h: h.memset(acc2[p][64:128, 0:64], 0.0), reads=[b_acc2[p]], writes=[b_acc2[p]], name="acc0")
                K.op("act", lambda h: h.activation(out=junk[:rows, :], in_=xs_[:rows, :], func=AF.Square, accum_out=ss[p][:rows, j:j + 1]),
                     reads=[b_xs, b_ss[p]], writes=[b_ss[p], b_junk], name="sqacc")
                K.op("dve", lambda h: h.tensor_scalar(out=acc2[p][:rows, j * 128:j * 128 + rows], in0=ident[:rows, :rows],
                                                      scalar1=ss[p][:rows, j:j + 1], scalar2=None, op0=ALU.mult),
                     reads=[b_ss[p], b_const, b_acc2[p]], writes=[b_acc2[p]], name="ssdiag")

            for tb in range(17):
                rows = 128 if tb < 16 else 64
                tt = tb // 4
                s2 = (tb // 2) % NX
                xs_ = xst2[s2][:, tb % 2, :]
                b_xs = b_xst2[s2]
                if tb % 2 == 0:
                    if tb < 16:
                        src = xin[tb * 128:tb * 128 + 256, :].rearrange("(j p) d -> p j d", p=128)
                        x_ops.append(K.op("sp" if (tb // 2) % 2 == 0 else "pool", lambda h, s2=s2, src=src: h.dma_start(out=xst2[s2][:, :, :], in_=src),
                                          writes=[b_xs], dma=True, name="xload"))
                    else:
                        x_ops.append(K.op("sp", lambda h, s2=s2: h.dma_start(out=xst2[s2][:64, 0, :], in_=xin[2048:2112, :]),
                                          writes=[b_xs], dma=True, name="xload"))
                for half in range(2):
                    bank, bb = getbank()

                    def tr(h, bank=bank, xs_=xs_, rows=rows, half=half):
                        ins = None
                        for q in range(4):
                            c = half * 4 + q
                            ins = h.transpose(out=bank[:, q * 128:q * 128 + rows], in_=xs_[:rows, c * 128:(c + 1) * 128],
                                              identity=ident[:rows, :rows])
                        return ins
                    K.op("pe", tr, reads=[b_xs, b_const], writes=[bb], name="xtr")
                    eng = "act" if half == 0 else "dve"

                    def ev(h, bank=bank, tb=tb, rows=rows, half=half, eng=eng):
                        src = bank[:, :].rearrange("p (q t) -> p q t", q=4)[:, :, :rows]
                        dst = xT[:, half * 4:half * 4 + 4, tb * 128:tb * 128 + rows]
                        if eng == "act":
                            return h.copy(out=dst, in_=src)
                        return h.tensor_copy(out=dst, in_=src)
                    K.op(eng, ev, reads=[bb], writes=[b_x[half * 4 + q][tt] for q in range(4)], name="xev")
                norm0_block(tt, tb, rows, xs_, b_xs)
                if tb % 4 == 3 or tb == 16:
                    if pend0[0] is not None:
                        pend0[0]()
                        pend0[0] = None
                    t0_, N_ = TT[tt]
                    p_ = tt % 2
                    p2 = (lambda tt=tt, t0_=t0_, N_=N_, p_=p_: norm_part2(0, tt, t0_, N_, acc2[p_], rs2[p_], b_acc2[p_], b_rs2[p_], None))
                    if tt >= 3:
                        pending_h[tt] = p2
                    else:
                        pend0[0] = p2
            nstate["n"] = 5
        K.fence()

        def conv_stage():
            with contextlib.ExitStack() as st:
                byT = T(st, "byT", [128, 8, NT], BF16)
                b_by = [[Buf() for _ in range(5)] for _ in range(8)]
                uf = [T(st, "uf0", [128, 2050], F32)] * 2
                us = [T(st, "us0", [128, 16, 6], F32)] * 2
                b_u = [[Buf() for _ in range(6)]] * 2
                tcp = [T(st, "tcp0", [128, 512], F32)] * 2
                b_tcp = [Buf()] * 2
                yt = [T(st, "yt0", [128, 512], F32)] * 2
                b_yt = [Buf()] * 2
                cst = T(st, "cst", [128, 8, 32], F32)
                cso = T(st, "cso", [128, 8, 34], F32)
                csr = T(st, "csr", [34, D], F32)
                scs = csr
                b_cst, b_cso, b_csr = Buf(), Buf(), Buf()
                b_scs = b_csr

                K.op("sp", lambda h: h.dma_start(out=scs[:32, :], in_=sconv), writes=[b_scs], dma=True)
                bank, bb = getbank()

                def trs(h, bank=bank):
                    ins = None
                    for c in range(8):
                        ins = h.transpose(out=bank[:, c * 32:(c + 1) * 32], in_=scs[:32, c * 128:(c + 1) * 128], identity=ident[:32, :32])
                    return ins
                K.op("pe", trs, reads=[b_scs, b_const], writes=[bb])
                K.op("act", lambda h, bank=bank: h.copy(out=cst[:, :, :], in_=bank[:, 0:256].rearrange("p (c s) -> p c s", c=8)),
                     reads=[bb], writes=[b_cst])
                cnt = 0
                for mp in range(4):
                    wc_, bwc = wcols(conv_w_in, D + mp * 256)
                    wh_, bwh = wcols(conv_w_in, 2 * D + mp * 256)
                    wb_, bwb = wcols(conv_w_in, mp * 256)
                    for q in range(2):
                        m = 2 * mp + q
                        ui = m % 2
                        ufm, usm, bu = uf[ui], us[ui], b_u[ui]
                        K.op("dve", lambda h, ufm=ufm: h.memset(ufm[:, 0:2], 0.0), writes=[bu[5]])
                        K.op("dve", lambda h, usm=usm, m=m: h.tensor_copy(out=usm[:, :, 0:2], in_=cst[:, m, :].rearrange("p (s j) -> p s j", j=2)),
                             reads=[b_cst], writes=[bu[4]])
                        for tt, (t0, N) in enumerate(TT):
                            need_h(tt)
                            sl = slice(q * 128, (q + 1) * 128)
                            bc, bbc = getbank()
                            mm_group(bc[:, :N], [(wc_[:, kc, sl], hT[:, kc, t0:t0 + N]) for kc in range(8)],
                                     [bwc] + [b_h[kc][tt] for kc in range(8)], bbc)
                            bh, bbh = getbank()
                            mm_group(bh[:, :N], [(wh_[:, kc, sl], hT[:, kc, t0:t0 + N]) for kc in range(8)],
                                     [bwh] + [b_h[kc][tt] for kc in range(8)], bbh)
                            bbk, bbb = getbank()
                            mm_group(bbk[:, :N], [(wb_[:, kc, sl], hT[:, kc, t0:t0 + N]) for kc in range(8)],
                                     [bwb] + [b_h[kc][tt] for kc in range(8)], bbb)
                            ti = cnt % 2
                            cnt += 1
                            K.op("act", lambda h, bc=bc, N=N, ti=ti: h.copy(out=tcp[ti][:, :N], in_=bc[:, :N]),
                                 reads=[bbc], writes=[b_tcp[ti]])
                            if tt < 4:
                                udst = ufm[:, 2 + t0:2 + t0 + N]
                                K.op("dve", lambda h, udst=udst, bh=bh, N=N, ti=ti: h.tensor_tensor(out=udst, in0=tcp[ti][:, :N], in1=bh[:, :N], op=ALU.mult),
                                     reads=[b_tcp[ti], bbh], writes=[bu[tt]])
                                u0, u1, u2 = ufm[:, t0:t0 + N], ufm[:, t0 + 1:t0 + 1 + N], ufm[:, t0 + 2:t0 + 2 + N]
                                ydst = yt[ti][:, :N]
                                rd = [bu[tt], bu[tt - 1] if tt > 0 else bu[5], b_const]
                            else:
                                udst = usm[:, :, 2:6]
                                K.op("dve", lambda h, udst=udst, bh=bh, ti=ti: h.tensor_tensor(
                                    out=udst, in0=tcp[ti][:, :64].rearrange("p (s t) -> p s t", t=4),
                                    in1=bh[:, :64].rearrange("p (s t) -> p s t", t=4), op=ALU.mult),
                                    reads=[b_tcp[ti], bbh, bu[4]], writes=[bu[4]])
                                u0, u1, u2 = usm[:, :, 0:4], usm[:, :, 1:5], usm[:, :, 2:6]
                                ydst = yt[ti][:, :64].rearrange("p (s t) -> p s t", t=4)
                                rd = [bu[4], b_const]
                            K.op("dve", lambda h, ydst=ydst, u2=u2, m=m: h.tensor_scalar(out=ydst, in0=u2, scalar1=wccol(2, m), scalar2=None, op0=ALU.mult),
                                 reads=rd, writes=[b_yt[ti]])
                            K.op("dve", lambda h, ydst=ydst, u1=u1, m=m: h.scalar_tensor_tensor(out=ydst, in0=u1, scalar=wccol(1, m), in1=ydst, op0=ALU.mult, op1=ALU.add),
                                 reads=rd + [b_yt[ti]], writes=[b_yt[ti]])
                            K.op("dve", lambda h, ydst=ydst, u0=u0, m=m: h.scalar_tensor_tensor(out=ydst, in0=u0, scalar=wccol(0, m), in1=ydst, op0=ALU.mult, op1=ALU.add),
                                 reads=rd + [b_yt[ti]], writes=[b_yt[ti]])
                            K.op("dve", lambda h, bbk=bbk, ti=ti, m=m, t0=t0, N=N: h.tensor_tensor(out=byT[:, m, t0:t0 + N], in0=yt[ti][:, :N], in1=bbk[:, :N], op=ALU.mult),
                                 reads=[b_yt[ti], bbb], writes=[b_by[m][tt]])
                        K.op("act", lambda h, ufm=ufm, m=m: h.copy(out=cso[:, m, 0:2], in_=ufm[:, 2048:2050]),
                             reads=[bu[3]], writes=[b_cso])
                        K.op("act", lambda h, usm=usm, m=m: h.copy(out=cso[:, m, 2:34].rearrange("p (s j) -> p s j", j=2), in_=usm[:, :, 4:6]),
                             reads=[bu[4]], writes=[b_cso])
                for half in range(2):
                    bank, bb = getbank()

                    def trc(h, bank=bank, half=half):
                        ins = None
                        for q in range(4):
                            c = half * 4 + q
                            ins = h.transpose(out=bank[:34, q * 128:(q + 1) * 128], in_=cso[:, c, :], identity=ident[:, :])
                        return ins
                    K.op("pe", trc, reads=[b_cso, b_const], writes=[bb])
                    K.op("act", lambda h, bank=bank, half=half: h.copy(out=csr[:34, half * 512:(half + 1) * 512], in_=bank[:34, :]),
                         reads=[bb], writes=[b_csr])
                K.op("sp", lambda h: h.dma_start(out=convo_d, in_=csr[:34, :]), reads=[b_csr], dma=True, name="convo")
                wos = [wcols(conv_w_out, mp * 256) for mp in range(4)]
                pendc = [None]
                for tt, (t0, N) in enumerate(TT):
                    for mo in range(8):
                        wo_, bwo = wos[mo // 2]
                        sl = slice((mo % 2) * 128, (mo % 2) * 128 + 128)
                        bank, bb = getbank()
                        mm_group(bank[:, :N], [(wo_[:, kc, sl], byT[:, kc, t0:t0 + N]) for kc in range(8)],
                                 [bwo] + [b_by[kc][tt] for kc in range(8)], bb)
                        residual_add(bank, bb, mo, tt, t0, N)
                        if mo == 3 and pendc[0] is not None:
                            pendc[0]()
                            pendc[0] = None
                    pendc[0] = norm_tile(1, tt, defer=True)
                pending_h[4] = pendc[0]
                K.fence()

        def ffn_stage(li):
            wg2, wu2, wd2 = ffn_w_gate[li], ffn_w_up[li], ffn_w_down[li]
            with contextlib.ExitStack() as st:
                actT = T(st, "actT", [128, 8, NT], BF16)
                b_a = [[Buf() for _ in range(5)] for _ in range(8)]
                sl_ = [T(st, f"silu{i}", [128, 512], F32) for i in range(2)]
                b_sl = [Buf(), Buf()]
                fcb = make_final_cb(st) if li == 1 else None
                cnt = 0
                for (f0, G) in ((0, 6), (6, 8), (14, 8)):
                    for fp in range(G // 2):
                        f = f0 + 2 * fp
                        wg_, bwg = wcols(wg2, f * 128)
                        wu_, bwu = wcols(wu2, f * 128)
                        for q in range(2):
                            fl = 2 * fp + q
                            sl = slice(q * 128, (q + 1) * 128)
                            for tt, (t0, N) in enumerate(TT):
                                need_h(tt)
                                bg, bbg = getbank()
                                mm_group(bg[:, :N], [(wg_[:, kc, sl], hT[:, kc, t0:t0 + N]) for kc in range(8)],
                                         [bwg] + [b_h[kc][tt] for kc in range(8)], bbg)
                                bu_, bbu = getbank()
                                mm_group(bu_[:, :N], [(wu_[:, kc, sl], hT[:, kc, t0:t0 + N]) for kc in range(8)],
                                         [bwu] + [b_h[kc][tt] for kc in range(8)], bbu)
                                ti = cnt % 2
                                cnt += 1
                                K.op("act", lambda h, bg=bg, N=N, ti=ti: h.activation(out=sl_[ti][:, :N], in_=bg[:, :N], func=AF.Silu),
                                     reads=[bbg], writes=[b_sl[ti]], name="silu")
                                K.op("dve", lambda h, bu_=bu_, N=N, ti=ti, fl=fl, t0=t0: h.tensor_tensor(
                                    out=actT[:, fl, t0:t0 + N], in0=sl_[ti][:, :N], in1=bu_[:, :N], op=ALU.mult),
                                    reads=[b_sl[ti], bbu], writes=[b_a[fl][tt]], name="actmul")
                                bgq.tick()
                    wds = [wrows(wd2, (f0 + 2 * j) * 128) for j in range(G // 2)]
                    lastg = (f0 == 14)

                    def down(mo, tt):
                        t0, N = TT[tt]
                        bank, bb = getbank()
                        pairs = [(wds[fl // 2][0][:, fl % 2, mo * 128:(mo + 1) * 128], actT[:, fl, t0:t0 + N]) for fl in range(G)]
                        mm_group(bank[:, :N], pairs, [w[1] for w in wds] + [b_a[fl][tt] for fl in range(G)], bb)
                        residual_add(bank, bb, mo, tt, t0, N)
                    if not lastg:
                        for mo in range(8):
                            for tt in range(5):
                                down(mo, tt)
                    else:
                        pendf = None
                        for tt in range(5):
                            for mo in range(8):
                                down(mo, tt)
                                if mo == 3 and pendf is not None:
                                    pendf()
                                    pendf = None
                            if li == 0:
                                pendf = norm_tile(2, tt, defer=True)
                            else:
                                pendf = norm_tile(4, tt, final_cb=fcb, defer=True)
                        if li == 0:
                            pending_h[4] = pendf
                        else:
                            pendf()
                bgq.drain()
                if li == 0:
                    K.fence()
                else:
                    K.barrier()

        def ret_stage():
            wq_all = ret_w_in
            HT = 1088
            halves = ((0, (0, 1)), (1024, (2, 3, 4)))
            with contextlib.ExitStack() as st:
                QK = T(st, "QK", [128, 4, HT], BF16)
                b_qk = [[Buf() for _ in range(9)] for _ in range(4)]
                ktok = T(st, "ktok", [128, 9, 256], BF16)
                b_kt = [Buf() for _ in range(9)]
                vtok = T(st, "vtok", [128, 9, 512], BF16)
                b_vt = [Buf() for _ in range(9)]
                tmp = [T(st, f"rtmp{i}", [128, 512], F32) for i in range(5)]
                b_tmp = [Buf() for _ in range(5)]
                tab = [T(st, f"tab{i}", [128, 512], F32) for i in range(2)]
                b_tab = [Buf() for _ in range(2)]
                PT = [T(st, f"PT{i}", [128, 128], BF16) for i in range(2)]
                b_PT = [Buf(), Buf()]
                PTs = T(st, "PTs", [64, 64], BF16)
                b_PTs = Buf()
                on = [T(st, f"on{i}", [128, 512], BF16) for i in range(2)]
                b_on = [Buf(), Buf()]
                stats = [T(st, f"stats{i}", [128, 8], F32) for i in range(2)]
                mv = [T(st, f"mv{i}", [128, 8], F32) for i in range(2)]
                b_stats, b_mv = [Buf(), Buf()], [Buf(), Buf()]
                Sb = T(st, "Sb", [128, 2, 512], BF16)
                b_Sb = Buf()
                S0b = [T(st, f"S0b{i}", [128, 2, 512], BF16) for i in range(2)]
                b_S0b = [Buf(), Buf()]
                S0b += [tab[j][:, :].bitcast(BF16).rearrange("p (a b) -> p a b", a=2) for j in range(2)]
                b_S0b += [b_tab[0], b_tab[1]]
                qpad = T(st, "qpad", [128, 2, 1024], BF16)
                b_qpad = Buf()
                gt = [tmp[3], tmp[4]]
                b_gt = [b_tmp[3], b_tmp[4]]

                K.op("dve", lambda h: h.memset(qpad[:, :, :], 0.0), writes=[b_qpad])
                Sf = T(st, "Sf", [128, 2, 512], F32)
                b_Sf = [Buf(), Buf()]

                cnt = {"tmp": 0, "on": 0, "gt": 0, "s0": 0, "pt": 0, "s0b": 0}
                pendr = [None]
                def head_half(hd, hi, tok0, tiles):
                    if True:
                        nblk = 8 if hi == 0 else 9
                        for which in (1, 0):
                            w_, bw = wcols(wq_all, which * D + hd * 256)
                            rope_d = ropeq_d if which == 0 else ropek_d
                            for tt in tiles:
                                need_h(tt)
                                t0, N = TT[tt]
                                l0 = t0 - tok0
                                lb = [l0 // 128 + i for i in range((N + 127) // 128)]
                                for j in range(2):
                                    K.op("sp", lambda h, j=j, t0=t0, N=N, rope_d=rope_d, which=which: h.dma_start(
                                        out=tab[j][:, :N], in_=rope_d[hd, j, :, t0:t0 + N]),
                                        writes=[b_tab[j]], dma=True, name="tab")
                                ct, sn = tab[0], tab[1]
                                bct, bsn = b_tab[0], b_tab[1]
                                b1, bb1 = getbank()
                                mm_group(b1[:, :N], [(w_[:, kc, 0:128], hT[:, kc, t0:t0 + N]) for kc in range(8)],
                                         [bw] + [b_h[kc][tt] for kc in range(8)], bb1)
                                b2, bb2 = getbank()
                                mm_group(b2[:, :N], [(w_[:, kc, 128:256], hT[:, kc, t0:t0 + N]) for kc in range(8)],
                                         [bw] + [b_h[kc][tt] for kc in range(8)], bb2)
                                t2, ta, tb_ = tmp[0], tmp[1], tmp[2]
                                K.op("act", lambda h, b2=b2, N=N: h.copy(out=tmp[0][:, :N], in_=b2[:, :N]), reads=[bb2], writes=[b_tmp[0]])
                                K.op("dve", lambda h, b1=b1, N=N, ct=ct: h.tensor_tensor(out=tmp[1][:, :N], in0=b1[:, :N], in1=ct[:, :N], op=ALU.mult),
                                     reads=[bb1, bct], writes=[b_tmp[1]])
                                K.op("dve", lambda h, N=N, sn=sn: h.tensor_tensor(out=tmp[2][:, :N], in0=tmp[0][:, :N], in1=sn[:, :N], op=ALU.mult),
                                     reads=[b_tmp[0], bsn], writes=[b_tmp[2]])
                                K.op("dve", lambda h, N=N, l0=l0, which=which: h.tensor_tensor(out=QK[:, 2 * which, l0:l0 + N], in0=tmp[1][:, :N], in1=tmp[2][:, :N], op=ALU.subtract),
                                     reads=[b_tmp[1], b_tmp[2]], writes=[b_qk[2 * which][b] for b in lb])
                                K.op("dve", lambda h, b1=b1, N=N, sn=sn: h.tensor_tensor(out=tmp[3][:, :N], in0=b1[:, :N], in1=sn[:, :N], op=ALU.mult),
                                     reads=[bb1, bsn], writes=[b_tmp[3]])
                                K.op("dve", lambda h, N=N, ct=ct: h.tensor_tensor(out=tmp[4][:, :N], in0=tmp[0][:, :N], in1=ct[:, :N], op=ALU.mult),
                                     reads=[b_tmp[0], bct], writes=[b_tmp[4]])
                                K.op("dve", lambda h, N=N, l0=l0, which=which: h.tensor_tensor(out=QK[:, 2 * which + 1, l0:l0 + N], in0=tmp[3][:, :N], in1=tmp[4][:, :N], op=ALU.add),
                                     reads=[b_tmp[3], b_tmp[4]], writes=[b_qk[2 * which + 1][b] for b in lb])
                                bgq.tick()
                        wv = [wcols(wq_all, 2 * D + hd * 512 + j * 256) for j in range(2)]
                        for lbk in range(nblk):
                            rows = 64 if (hi == 1 and lbk == 8) else 128
                            g0 = tok0 + lbk * 128
                            tt = g0 // 512
                            bank, bb = getbank()
                            for j in range(2):
                                mm_group(bank[:rows, j * 256:(j + 1) * 256],
                                         [(hT[:, kc, g0:g0 + rows], wv[j][0][:, kc, :]) for kc in range(8)],
                                         [wv[j][1]] + [b_h[kc][tt] for kc in range(8)], bb, name="vproj")
                            K.op("act", lambda h, bank=bank, lbk=lbk, rows=rows: h.copy(out=vtok[:rows, lbk, :], in_=bank[:rows, :]),
                                 reads=[bb], writes=[b_vt[lbk]], name="vtok")
                            bgq.tick()
                        groups = [list(range(0, 4)), list(range(4, 8))] + ([[8]] if hi == 1 else [])
                        for grp in groups:
                            rows = 64 if grp == [8] else 128
                            ptbs["n"] += 1

                            def trk(h, grp=grp, rows=rows):
                                ins = None
                                for i, lbk in enumerate(grp):
                                    for kc in range(2):
                                        ins = h.transpose(out=ptb[:rows, i * 256 + kc * 128:i * 256 + (kc + 1) * 128],
                                                          in_=QK[:, 2 + kc, lbk * 128:lbk * 128 + rows], identity=identb[:, :])
                                return ins
                            K.op("pe", trk, reads=[b_qk[2][l] for l in grp] + [b_qk[3][l] for l in grp] + [b_const], writes=[b_ptb[0]], name="trk")
                            n = len(grp)
                            K.op("act", lambda h, grp=grp, rows=rows, n=n: h.copy(out=ktok[:rows, grp[0]:grp[0] + n, :],
                                                                                 in_=ptb[:rows, 0:n * 256].rearrange("p (b d) -> p b d", b=n)),
                                 reads=[b_ptb[0]], writes=[b_kt[l] for l in grp], name="ktok")
                        def chunk(lbk, rows=128, bo_=None):
                            c0 = lbk * 128
                            first = (hi == 0 and lbk == 0)
                            last = (hi == 1 and lbk == 7)
                            blk = slice(c0, c0 + rows)
                            pi = cnt["pt"] % 2
                            cnt["pt"] += 1
                            if bo_ is None:
                                bo, bbo = pbank[4 + pi], b_pbank[4 + pi]
                            else:
                                bo, bbo = bo_
                            kvb = []

                            def A1():
                                bs, bbs = getbank()
                                mm_group(bs[:rows, :rows], [(QK[:, 2 + kc, blk], QK[:, kc, blk]) for kc in range(2)],
                                         [b_qk[a_][lbk] for a_ in range(4)], bbs, name="scoresT")
                                K.op("dve", lambda h: h.tensor_tensor(out=PT[pi][:rows, :rows], in0=bs[:rows, :rows], in1=maskT[:rows, :rows], op=ALU.mult),
                                     reads=[bbs, b_const], writes=[b_PT[pi]], name="PT")

                            def A2():
                                pairs = [(PT[pi][:rows, :rows], vtok[:rows, lbk, :])]
                                rd = [b_PT[pi], b_vt[lbk]]
                                if not first:
                                    pairs += [(QK[:, kc, blk], Sb[:, kc, :]) for kc in range(2)]
                                    rd += [b_Sb, b_qk[0][lbk], b_qk[1][lbk]]
                                mm_group(bo[:rows, :], pairs, rd, bbo, name="o")

                            def AK():
                                for kc in range(2):
                                    bk, bbk = getbank()
                                    kvb.append((bk, bbk))
                                    K.op("pe", lambda h, bk=bk, kc=kc: h.matmul(
                                        bk[:, :], lhsT=ktok[:, lbk, kc * 128:(kc + 1) * 128], rhs=vtok[:, lbk, :], start=True, stop=True),
                                        reads=[b_kt[lbk], b_vt[lbk]], writes=[bbk], name="kv")

                            def A3():
                                for kc in range(2):
                                    bk, bbk = kvb[kc]
                                    if first:
                                        K.op("dve", lambda h, bk=bk, kc=kc: h.tensor_copy(out=Sf[:, kc, :], in_=bk[:, :]),
                                             reads=[bbk], writes=[b_Sf[kc]], name="Sinit")
                                    else:
                                        K.op("dve", lambda h, bk=bk, kc=kc: h.tensor_tensor(out=Sf[:, kc, :], in0=Sf[:, kc, :], in1=bk[:, :], op=ALU.add),
                                             reads=[bbk, b_Sf[kc]], writes=[b_Sf[kc]], name="Sacc")

                            def A4():
                                if not last:
                                    K.op("act", lambda h: h.copy(out=Sb[:, :, :], in_=Sf[:, :, :]), reads=[b_Sf[0], b_Sf[1]], writes=[b_Sb], name="Sb")
                                else:
                                    for kc in range(2):
                                        K.op("act", lambda h, kc=kc: h.activation(out=Sf[:, kc, :], in_=Sf[:, kc, :], func=AF.Copy, scale=g2048[hd]),
                                             reads=[b_Sf[kc]], writes=[b_Sf[kc]], name="retp")
                                    K.op("sp", lambda h: h.dma_start(out=retp_d[hd].rearrange("(kc p) v -> p kc v", p=128), in_=Sf[:, :, :]),
                                         reads=[b_Sf[0], b_Sf[1]], dma=True, name="retp_out")

                            oi = cnt["on"] % 2
                            cnt["on"] += 1
                            st_, mv_, bst_, bmv_ = stats[oi], mv[oi], b_stats[oi], b_mv[oi]

                            def B1():
                                K.op("dve", lambda h: h.bn_stats(out=st_[:rows, 0:6], in_=bo[:rows, :]), reads=[bbo], writes=[bst_], name="bnst")
                                K.op("dve", lambda h: h.bn_aggr(out=mv_[:rows, 0:2], in_=st_[:rows, 0:6]), reads=[bst_, bmv_], writes=[bmv_], name="bnag")
                                K.op("act", lambda h: h.activation(out=mv_[:rows, 2:3], in_=mv_[:rows, 1:2], func=AF.Sqrt, bias=epsc[:rows, 1:2], scale=1.0),
                                     reads=[bmv_, b_const], writes=[bmv_], name="gnsqrt")

                            def B2():
                                K.op("dve", lambda h: h.reciprocal(out=mv_[:rows, 3:4], in_=mv_[:rows, 2:3]), reads=[bmv_], writes=[bmv_], name="gnrec")
                                K.op("dve", lambda h: h.tensor_scalar(out=mv_[:rows, 4:5], in0=mv_[:rows, 0:1], scalar1=mv_[:rows, 3:4], scalar2=-1.0,
                                                                      op0=ALU.mult, op1=ALU.mult),
                                     reads=[bmv_], writes=[bmv_], name="gnnb")
                                K.op("act", lambda h: h.activation(out=on[oi][:rows, :], in_=bo[:rows, :], func=AF.Identity, bias=mv_[:rows, 4:5], scale=mv_[:rows, 3:4]),
                                     reads=[bbo, bmv_], writes=[b_on[oi]], name="gnorm")

                            ph = ptbs["n"] % 2
                            ptbs["n"] += 1

                            def B3():
                                def tro(h):
                                    ins = None
                                    for m in range(4):
                                        ins = h.transpose(out=ptb[:, ph * 512 + m * 128:ph * 512 + m * 128 + rows], in_=on[oi][:rows, m * 128:(m + 1) * 128],
                                                          identity=identb[:rows, :rows])
                                    return ins
                                K.op("pe", tro, reads=[b_on[oi], b_const], writes=[b_ptb[ph]], name="tro")

                            def B4():
                                K.op("act", lambda h: h.copy(out=QK[:, :, c0:c0 + rows],
                                                             in_=ptb[:, ph * 512:ph * 512 + 512].rearrange("p (m t) -> p m t", m=4)[:, :, :rows]),
                                     reads=[b_ptb[ph]], writes=[b_qk[a_][lbk] for a_ in range(4)], name="oT")
                            return dict(A1=A1, AK=AK, A2=A2, A3=A3, A4=A4, B1=B1, B2=B2, B3=B3, B4=B4)

                        def state_load(s, par):
                            si = s % NS0
                            km, bkm = kmk[s % 2], b_kmk[s % 2]
                            K.op("sp", lambda h: h.dma_start(out=sS0f[si][:, :, :], in_=sret[s, hd].rearrange("(kc p) v -> p kc v", p=128)),
                                 writes=[b_sS0f[si]], dma=True, name="s0f")
                            K.op("dve", lambda h: h.tensor_scalar(out=km[:64, :], in0=sk[par][:64, :], scalar1=kmcol[:, hd * 16 + s:hd * 16 + s + 1],
                                                                  scalar2=None, op0=ALU.mult),
                                 reads=[b_skv[par], b_const], writes=[bkm], name="kmask")

                        def state_comp(s, par):
                            si = s % NS0
                            km, bkm = kmk[s % 2], b_kmk[s % 2]
                            for kc in range(2):
                                bk, bbk = getbank()
                                K.op("pe", lambda h, bk=bk, kc=kc: h.matmul(bk[:, :], lhsT=km[:64, kc * 128:(kc + 1) * 128], rhs=sv[par][:64, :], start=True, stop=True),
                                     reads=[bkm, b_skv[par]], writes=[bbk], name="kv_s")
                                K.op("dve", lambda h, bk=bk, kc=kc: h.scalar_tensor_tensor(
                                    out=sS0f[si][:, kc, :], in0=sS0f[si][:, kc, :], scalar=g4[hd], in1=bk[:, :], op0=ALU.mult, op1=ALU.add),
                                    reads=[bbk, b_sS0f[si]], writes=[b_sS0f[si]], name="snew")
                            K.op("sp", lambda h: h.dma_start(out=rets_d[s, hd].rearrange("(kc p) v -> p kc v", p=128), in_=sS0f[si][:, :, :]),
                                 reads=[b_sS0f[si]], dma=True, name="rets_out")

                        def s0b_load(s):
                            si = s % 4
                            K.op("pool", lambda h: h.dma_start(out=S0b[si][:, :, :], in_=sret[s, hd].rearrange("(kc p) v -> p kc v", p=128)),
                                 writes=[b_S0b[si]], dma=True, name="s0b")

                        def cross_seq(s, bo6):
                            si = s % 4

                            def cross(h):
                                ins = None
                                for kc in range(2):
                                    ins = h.matmul(bo6[:64, :], lhsT=qpad[:, kc, s * 64:(s + 1) * 64], rhs=S0b[si][:, kc, :], start=False, stop=(s == 15 and kc == 1))
                                return ins
                            K.op("pe", cross, reads=[b_qpad, b_S0b[si]], writes=[b_pbank[6]], name="cross_s")

                        pstate["avail"] = [0, 1, 2, 3]
                        if hi == 1:
                            par = hd % 2
                            bo6 = pbank[6]
                            sblk = slice(1024, 1088)
                            bs, bbs = getbank()
                            mm_group(bs[:64, :64], [(QK[:, 2 + kc, sblk], QK[:, kc, sblk]) for kc in range(2)],
                                     [b_qk[a_][8] for a_ in range(4)], bbs, name="scoresT_s")
                            K.op("dve", lambda h: h.tensor_tensor(out=PTs[:64, :64], in0=bs[:64, :64], in1=smask[:64, :64], op=ALU.mult),
                                 reads=[bbs, b_const], writes=[b_PTs], name="PTs")
                            K.op("pe", lambda h: h.matmul(bo6[:64, :], lhsT=PTs[:64, :64], rhs=vtok[:64, 8, :], start=True, stop=False),
                                 reads=[b_PTs, b_vt[8]], writes=[b_pbank[6]], name="o_s_inner")
                            for kc in range(2):
                                K.op("dve", lambda h, kc=kc: h.tensor_copy(
                                    out=qpad[:, kc, 0:1020].rearrange("p (s e) -> p s e", e=68)[:, :, 0:4],
                                    in_=QK[:, kc, 1024:1084].rearrange("p (s t) -> p s t", t=4)),
                                    reads=[b_qk[kc][8], b_qpad], writes=[b_qpad], name="qpad")
                                K.op("dve", lambda h, kc=kc: h.tensor_copy(out=qpad[:, kc, 1020:1024], in_=QK[:, kc, 1084:1088]),
                                     reads=[b_qk[kc][8], b_qpad], writes=[b_qpad], name="qpad2")
                            K.op("act", lambda h: h.copy(out=sk[par][:64, :], in_=ktok[:64, 8, :]), reads=[b_kt[8]], writes=[b_skv[par]], name="skcp")
                            K.op("act", lambda h: h.copy(out=sv[par][:64, :], in_=vtok[:64, 8, :]), reads=[b_vt[8], b_skv[par]], writes=[b_skv[par]], name="svcp")
                            bgq.add(lambda: state_load(0, par))
                            for s_ in range(16):
                                if s_ + 1 < 16:
                                    bgq.add(lambda s_=s_: state_load(s_ + 1, par))
                                bgq.add(lambda s_=s_: state_comp(s_, par))
                        chs = [chunk(lbk) for lbk in range(8)]
                        nop = lambda: None
                        chs[0]["A1"](); chs[0]["AK"](); chs[0]["A2"](); chs[0]["A3"](); chs[0]["A4"]()
                        if hi == 1:
                            for s_ in range(4):
                                s0b_load(s_)
                        for ci in range(8):
                            cur = chs[ci]
                            nxt = chs[ci + 1] if ci + 1 < 8 else dict(A1=nop, AK=nop, A2=nop, A3=nop, A4=nop)
                            cur["B1"]()
                            nxt["A1"]()
                            nxt["AK"]()
                            nxt["A2"]()
                            cur["B2"]()
                            nxt["A3"]()
                            nxt["A4"]()
                            cur["B3"]()
                            cur["B4"]()
                            if hi == 1:
                                cross_seq(2 * ci, bo6)
                                cross_seq(2 * ci + 1, bo6)
                                if ci < 6:
                                    s0b_load(2 * ci + 4)
                                    s0b_load(2 * ci + 5)
                            bgq.tick()
                        if hi == 1:
                            sc = chunk(8, rows=64, bo_=(bo6, b_pbank[6]))
                            sc["B1"](); sc["B2"](); sc["B3"](); sc["B4"]()
                        pstate["avail"] = [0, 1, 2, 3, 4, 5, 6]

                        wg = [wcols(wq_all, 4 * D + hd * 512 + j * 256) for j in range(2)]
                        for m in range(4):
                            w_, bw = wg[m // 2]
                            sl = slice((m % 2) * 128, (m % 2) * 128 + 128)
                            for tt in tiles:
                                t0, N = TT[tt]
                                l0 = t0 - tok0
                                lb = [l0 // 128 + i for i in range((N + 127) // 128)]
                                bank, bb = getbank()
                                mm_group(bank[:, :N], [(w_[:, kc, sl], hT[:, kc, t0:t0 + N]) for kc in range(8)],
                                         [bw] + [b_h[kc][tt] for kc in range(8)], bb, name="gate")
                                gi = cnt["gt"] % 2
                                cnt["gt"] += 1
                                K.op("act", lambda h, bank=bank, N=N, gi=gi: h.activation(out=gt[gi][:, :N], in_=bank[:, :N], func=AF.Silu),
                                     reads=[bb], writes=[b_gt[gi]], name="gsilu")
                                K.op("dve", lambda h, m=m, l0=l0, N=N, gi=gi: h.scalar_tensor_tensor(
                                    out=QK[:, m, l0:l0 + N], in0=QK[:, m, l0:l0 + N], scalar=gncol(hd * 4 + m), in1=gt[gi][:, :N], op0=ALU.mult, op1=ALU.mult),
                                    reads=[b_gt[gi], b_const] + [b_qk[m][b] for b in lb], writes=[b_qk[m][b] for b in lb], name="gating")
                                bgq.tick()
                        wo = [wrows(ret_w_out, hd * 512 + j * 256) for j in range(2)]
                        for tt in tiles:
                            for mo in range(8):
                                t0, N = TT[tt]
                                l0 = t0 - tok0
                                lb = [l0 // 128 + i for i in range((N + 127) // 128)]
                                bank, bb = getbank()
                                pairs = [(wo[m // 2][0][:, m % 2, mo * 128:(mo + 1) * 128], QK[:, m, l0:l0 + N]) for m in range(4)]
                                mm_group(bank[:, :N], pairs, [w[1] for w in wo] + [b_qk[m][b] for m in range(4) for b in lb], bb, name="wout")
                                residual_add(bank, bb, mo, tt, t0, N)
                                bgq.tick()
                            if hd == NH - 1:
                                if pendr[0] is not None:
                                    pendr[0]()
                                pendr[0] = norm_tile(3, tt, defer=True)
                        if hd == NH - 1:
                            if hi == 1:
                                pending_h[4] = pendr[0]
                            else:
                                pendr[0]()
                            pendr[0] = None
                for hd_ in range(NH):
                    for hi_, (tok0_, tiles_) in enumerate(halves):
                        head_half(hd_, hi_, tok0_, tiles_)
                pstate["avail"] = [0, 1, 2, 3, 4, 5, 6]
                pstate["next"] = 0
                K.fence()

        def make_final_cb(st):
            yT = T(st, "yT", [128, 8, 256], F32)
            b_y = [Buf() for _ in range(8)]
            yst = [T(st, f"yst{i}", [128, D], F32) for i in range(2)]
            b_yst = [Buf(), Buf()]
            state = {"n": 0}

            def cb(tt, t0, N, rs_, brs_):
                for h0 in range(0, N, 256):
                    n = min(256, N - h0)
                    for c in range(8):
                        K.op("dve", lambda h, c=c, h0=h0, n=n: h.scalar_tensor_tensor(out=yT[:, c, :n], in0=xT[:, c, t0 + h0:t0 + h0 + n], scalar=gcol(4, c),
                                                                                      in1=rs_[:, h0:h0 + n], op0=ALU.mult, op1=ALU.mult),
                             reads=[b_x[c][tt], brs_, b_const], writes=[b_y[c]], name="ynorm")
                    for i in range((n + 127) // 128):
                        rows = min(128, n - i * 128)
                        tb = (t0 + h0) // 128 + i
                        s_ = state["n"] % 2
                        state["n"] += 1
                        for half in range(2):
                            bank, bb = getbank()

                            def tr(h, bank=bank, i=i, rows=rows, half=half):
                                ins = None
                                for q in range(4):
                                    cc = half * 4 + q
                                    ins = h.transpose(out=bank[:rows, q * 128:(q + 1) * 128], in_=yT[:, cc, i * 128:i * 128 + rows], identity=ident[:, :])
                                return ins
                            K.op("pe", tr, reads=[b_y[half * 4 + q] for q in range(4)] + [b_const], writes=[bb], name="ytr")
                            eng = "act" if half == 0 else "dve"

                            def ev(h, bank=bank, s_=s_, rows=rows, half=half, eng=eng):
                                if eng == "act":
                                    return h.copy(out=yst[s_][:rows, half * 512:(half + 1) * 512], in_=bank[:rows, :])
                                return h.tensor_copy(out=yst[s_][:rows, half * 512:(half + 1) * 512], in_=bank[:rows, :])
                            K.op(eng, ev, reads=[bb], writes=[b_yst[s_]], name="yev")
                        K.op("sp", lambda h, s_=s_, tb=tb, rows=rows: h.dma_start(out=y_d[tb * 128:tb * 128 + rows, :], in_=yst[s_][:rows, :]),
                             reads=[b_yst[s_]], dma=True, name="yout")
            return cb

        def final_stage():
            with contextlib.ExitStack() as st:
                cb = make_final_cb(st)
                for tt in range(5):
                    norm_tile(4, tt, final_cb=cb)
                K.barrier()

        stages = [("conv", conv_stage), ("ffn0", lambda: ffn_stage(0)), ("ret", ret_stage), ("ffn1", lambda: ffn_stage(1))]
        if stop_after == "load":
            stages = []
        for nm, fnc in stages:
            fnc()
            if stop_after == nm:
                break
        if stop_after is not None:
            final_stage()
        K.emit()
    return nc, K


_CACHE = {}


def kernel(x_prompt, x_sample, state_conv, state_ret, norm_mix, norm_ffn, conv_w_in, conv_w,
           conv_w_out, ret_w_in, ret_gn, ret_w_out, ffn_w_gate, ffn_w_up, ffn_w_down, final_norm,
           _stop_after=None):
    f = lambda a: np.ascontiguousarray(np.asarray(a, dtype=np.float32))
    x_prompt, x_sample, state_conv, state_ret = f(x_prompt), f(x_sample), f(state_conv), f(state_ret)
    consts, g4, g2048 = _consts()
    key = ("nc", _stop_after)
    if key not in _CACHE:
        _CACHE[key] = build_nc(g4, g2048, stop_after=_stop_after)
    nc, _ = _CACHE[key]
    def colify(v):
        return f(v).reshape(-1, 128).T
    cols = np.concatenate([colify(norm_mix[0]), colify(norm_ffn[0]), colify(norm_mix[1]), colify(norm_ffn[1]),
                           colify(final_norm), colify(f(conv_w)[0, 0]), colify(f(conv_w)[0, 1]), colify(f(conv_w)[0, 2]),
                           colify(f(ret_gn)[0])], axis=1)
    cols = np.ascontiguousarray(cols, dtype=np.float32)
    shared = dict(cols=cols, conv_w_in=f(conv_w_in)[0], conv_w_out=f(conv_w_out)[0], ret_w_in=f(ret_w_in)[0],
                  ret_w_out=f(ret_w_out)[0], ffn_w_gate=f(ffn_w_gate), ffn_w_up=f(ffn_w_up), ffn_w_down=f(ffn_w_down), **consts)
    in_maps = []
    for c in range(8):
        m = dict(shared)
        m["xin"] = np.ascontiguousarray(np.concatenate([x_prompt[c], x_sample[16 * c:16 * c + 16].reshape(64, D)], axis=0))
        m["sconv"] = np.ascontiguousarray(state_conv[0, 16 * c:16 * c + 16].reshape(32, D))
        m["sret"] = np.ascontiguousarray(state_ret[0, 16 * c:16 * c + 16])
        in_maps.append(m)
    res = run_bass_kernel_spmd(nc, in_maps, core_ids=list(range(8)))
    R = res.results
    y_prompt = np.stack([R[c]["y"][:2048] for c in range(8)], axis=0)
    y_sample = np.concatenate([R[c]["y"][2048:].reshape(16, 4, D) for c in range(8)], axis=0)
    conv_prompt = np.stack([R[c]["convo"][0:2] for c in range(8)], axis=0)[None]
    conv_sample = np.concatenate([R[c]["convo"][2:34].reshape(16, 2, D) for c in range(8)], axis=0)[None]
    ret_prompt = np.stack([R[c]["retp"] for c in range(8)], axis=0)[None]
    ret_sample = np.concatenate([R[c]["rets"] for c in range(8)], axis=0)[None]
    return (y_prompt.astype(np.float32), y_sample.astype(np.float32), conv_prompt.astype(np.float32),
            conv_sample.astype(np.float32), ret_prompt.astype(np.float32), ret_sample.astype(np.float32))
```

```python
import contextlib
import numpy as np
import concourse.bass as bass
import concourse.mybir as mybir
from concourse.bass_utils import run_bass_kernel_spmd

F32 = mybir.dt.float32
BF16 = mybir.dt.bfloat16
AF = mybir.ActivationFunctionType
ALU = mybir.AluOpType

D = 1024
NT = 2112
TT = [(0, 512), (512, 512), (1024, 512), (1536, 512), (2048, 64)]
DFF = 2816
NH = 4
RMS_EPS = 1e-6
GN_EPS = 1e-6
ENGS = ("pe", "act", "dve", "pool", "sp")
NSLOT = 6


class Buf:
    __slots__ = ("name", "w", "r", "fence")
    FENCE = None

    def __init__(self, name=""):
        self.name = name
        self.w = None
        self.r = []
        self.fence = Buf.FENCE


class Op:
    __slots__ = ("eng", "fn", "deps", "dma", "sig", "sem", "val", "pre", "name")


class Kern:
    def __init__(self, nc, n_dma_sems=16):
        self.nc = nc
        self.ops = []
        self.n_dma_sems = n_dma_sems
        self.last = {e: None for e in ENGS}
        self.dmas = []

    def op(self, eng, fn, reads=(), writes=(), dma=False, name="", extra=()):
        o = Op()
        o.eng, o.fn, o.dma, o.name = eng, fn, dma, name
        o.sig, o.sem, o.val, o.pre = False, None, 0, None
        deps = set(extra)
        for b in reads:
            if b.w is not None:
                deps.add(b.w)
            if b.fence is not None:
                deps.update(b.fence)
        for b in writes:
            if b.w is not None:
                deps.add(b.w)
            deps.update(b.r)
            if b.fence is not None:
                deps.update(b.fence)
                b.fence = None
        if eng == "pe":
            deps = {d for d in deps if not (d.eng == "pe" and not d.dma)}
        o.deps = deps
        for b in reads:
            b.r.append(o)
        for b in writes:
            b.w = o
            b.r = []
        self.ops.append(o)
        if dma:
            self.dmas.append(o)
        else:
            self.last[eng] = o
        return o

    def fence(self):
        fr = [o for o in self.last.values() if o is not None] + list(self.dmas)
        Buf.FENCE = fr

    def barrier(self):
        ex = [o for o in self.last.values() if o is not None] + list(self.dmas)
        self.dmas = []
        saved = dict(self.last)
        for e in ENGS:
            self.op(e, lambda h: None, extra=ex, name="barrier")
        self.last = saved

    def emit(self):
        nc = self.nc
        ops = self.ops
        for o in ops:
            if o.dma:
                o.sig = True
            for d in o.deps:
                d.sig = True
        with contextlib.ExitStack() as st:
            esem = {e: st.enter_context(nc.semaphore(f"s_{e}")) for e in ENGS if e != "sp"}
            nd = self.n_dma_sems
            dsem = {e: [st.enter_context(nc.semaphore(f"s_dma_{e}{i}")) for i in range(nd)] for e in ("sp", "pool", "act")}
            cnt = {e: 0 for e in ENGS}
            ndma = {e: 0 for e in ENGS}
            for o in ops:
                if o.dma:
                    k = ndma[o.eng]
                    ndma[o.eng] += 1
                    o.sem = dsem[o.eng][k % nd]
                    o.val = 16 * (k // nd + 1)
                    o.pre = (o.sem, o.val - 16) if o.val > 16 else None
                elif o.sig:
                    assert o.eng != "sp"
                    cnt[o.eng] += 1
                    o.sem = esem[o.eng]
                    o.val = cnt[o.eng]
            streams = {e: [o for o in ops if o.eng == e] for e in ENGS}
            self.stats = {e: len(streams[e]) for e in ENGS}
            self.stats["sig"] = dict(cnt)
            block = st.enter_context(nc.Block())

            def run(e, h):
                known = {}
                for o in streams[e]:
                    need = {}
                    for d in o.deps:
                        k = id(d.sem)
                        if need.get(k, (None, 0))[1] < d.val:
                            need[k] = (d.sem, d.val)
                    if o.pre is not None:
                        k = id(o.pre[0])
                        if need.get(k, (None, 0))[1] < o.pre[1]:
                            need[k] = o.pre
                    for k, (s, v) in need.items():
                        if known.get(k, 0) < v:
                            h.wait_ge(s, v)
                            known[k] = v
                    ins = o.fn(h)
                    if o.sig:
                        assert ins is not None, o.name
                        ins.then_inc(o.sem, 16 if o.dma else 1)

            @block.tensor
            def _(h):
                run("pe", h)

            @block.scalar
            def _(h):
                run("act", h)

            @block.vector
            def _(h):
                run("dve", h)

            @block.gpsimd
            def _(h):
                run("pool", h)

            @block.sync
            def _(h):
                run("sp", h)


def _consts():
    half = 128
    inv = 10000.0 ** (-(np.arange(half, dtype=np.float64) / half))
    pos = np.concatenate([np.arange(2048, dtype=np.float64), np.tile(16384.0 + np.arange(4, dtype=np.float64), 16)])
    ang = pos[None, :] * inv[:, None]
    cos = np.cos(ang)
    sin = np.sin(ang)
    e = np.concatenate([np.arange(2048, dtype=np.float64), np.tile(np.arange(4, dtype=np.float64), 16)]) + 1.0
    g = np.array([1.0 - 2.0 ** (-5.0 - h) for h in range(NH)], dtype=np.float64)
    lg = np.log(g.astype(np.float32)).astype(np.float64)
    ropeq = np.zeros((NH, 2, 128, NT), np.float32)
    ropek = np.zeros((NH, 2, 128, NT), np.float32)
    for h in range(NH):
        sq = np.exp(e * lg[h])[None, :]
        sk = np.exp(-e * lg[h])[None, :] * (256.0 ** -0.5)
        ropeq[h, 0] = cos * sq
        ropeq[h, 1] = sin * sq
        ropek[h, 0] = cos * sk
        ropek[h, 1] = sin * sk
    j = np.arange(128)
    maskT = (j[:, None] <= j[None, :]).astype(np.float32)
    js = np.arange(64)
    smask = ((js[:, None] // 4 == js[None, :] // 4) & (js[:, None] <= js[None, :])).astype(np.float32)
    kmcol = np.zeros((64, NH * 16), np.float32)
    for h in range(NH):
        for s in range(16):
            kmcol[4 * s:4 * s + 4, h * 16 + s] = np.exp(4.0 * lg[h])
    g4 = [float(np.exp(4.0 * lg[h])) for h in range(NH)]
    g2048 = [float(np.exp(2048.0 * lg[h])) for h in range(NH)]
    return dict(ropeq=ropeq, ropek=ropek, maskT=maskT, smask=smask, kmcol=kmcol,
                ident=np.eye(128, dtype=np.float32), ones=np.ones((128, 128), np.float32)), g4, g2048


def build_nc(g4, g2048, stop_after=None):
    nc = bass.Bass("TRN2", target_bir_lowering=False)

    def din(name, shape):
        return nc.dram_tensor(name, list(shape), F32, kind="ExternalInput").ap()

    def dout(name, shape):
        return nc.dram_tensor(name, list(shape), F32, kind="ExternalOutput").ap()

    xin = din("xin", [NT, D])
    sconv = din("sconv", [32, D])
    sret = din("sret", [16, NH, 256, 512])
    cols_d = din("cols", [128, 80])
    conv_w_in = din("conv_w_in", [D, 3 * D])
    conv_w_out = din("conv_w_out", [D, D])
    ret_w_in = din("ret_w_in", [D, 6 * D])
    ret_w_out = din("ret_w_out", [2 * D, D])
    ffn_w_gate = din("ffn_w_gate", [2, D, DFF])
    ffn_w_up = din("ffn_w_up", [2, D, DFF])
    ffn_w_down = din("ffn_w_down", [2, DFF, D])
    ropeq_d = din("ropeq", [NH, 2, 128, NT])
    ropek_d = din("ropek", [NH, 2, 128, NT])
    maskT_d = din("maskT", [128, 128])
    smask_d = din("smask", [64, 64])
    kmcol_d = din("kmcol", [64, NH * 16])
    ident_d = din("ident", [128, 128])
    ones_d = din("ones", [128, 128])
    y_d = dout("y", [NT, D])
    convo_d = dout("convo", [34, D])
    retp_d = dout("retp", [NH, 256, 512])
    rets_d = dout("rets", [16, NH, 256, 512])

    Buf.FENCE = None
    K = Kern(nc)
    with contextlib.ExitStack() as top:
        uniq = {"n": 0}

        def T(st, name, shape, dt):
            uniq["n"] += 1
            return st.enter_context(nc.sbuf_tensor(f"sb{uniq['n']}_{name}", list(shape), dt))

        def PS(name, shape, dt):
            return top.enter_context(nc.psum_tensor("ps_" + name, list(shape), dt))

        xT = T(top, "xT", [128, 8, NT], F32)
        hT = T(top, "hT", [128, 8, NT], BF16)
        ring = [T(top, f"ring{i}", [128, 2048], BF16) for i in range(NSLOT)]
        ident = T(top, "ident", [128, 128], F32)
        identb = T(top, "identb", [128, 128], BF16)
        ones = T(top, "ones", [128, 128], F32)
        cols = T(top, "colsb", [128, 80], F32)
        maskT = T(top, "maskT", [128, 128], F32)
        smask = T(top, "smask", [64, 64], F32)
        kmcol = T(top, "kmcol", [64, NH * 16], F32)
        acc2 = [T(top, f"acc{i}", [128, 512], F32) for i in range(2)]
        sq = [T(top, f"sq{i}", [128, 512], F32) for i in range(2)]
        rs2 = [T(top, f"rs{i}", [128, 512], F32) for i in range(2)]

        NS0 = 2
        sS0f = [T(top, f"sS0f{i}", [128, 2, 512], F32) for i in range(NS0)]
        b_sS0f = [Buf() for _ in range(NS0)]
        sk = [T(top, f"sk{i}", [64, 256], BF16) for i in range(2)]
        sv = [T(top, f"sv{i}", [64, 512], BF16) for i in range(2)]
        b_skv = [Buf(), Buf()]
        kmk = [T(top, f"kmk{i}", [64, 256], BF16) for i in range(2)]
        b_kmk = [Buf(), Buf()]

        class BG:
            def __init__(self):
                self.q = []
                self.n = 0
                self.stride = 3

            def add(self, fn):
                self.q.append(fn)

            def tick(self):
                self.n += 1
                if self.q and self.n % self.stride == 0:
                    self.q.pop(0)()

            def drain(self):
                while self.q:
                    self.q.pop(0)()
        bgq = BG()

        b_x = [[Buf(f"x{c}_{t}") for t in range(5)] for c in range(8)]
        b_h = [[Buf(f"h{c}_{t}") for t in range(5)] for c in range(8)]
        b_ring = [Buf(f"ring{i}") for i in range(NSLOT)]
        b_const = Buf("const")
        b_acc2, b_rs2 = [Buf(), Buf()], [Buf(), Buf()]
        b_sq = [Buf("sq0"), Buf("sq1")]

        pbank = [PS(f"pb{i}", [128, 512], F32) for i in range(7)]
        b_pbank = [Buf(f"pb{i}") for i in range(7)]
        ptb = PS("ptb", [128, 1024], BF16)
        _bp = Buf("ptb")
        b_ptb = [_bp, _bp]
        ptbs = {"n": 0}
        pstate = {"next": 0, "avail": [0, 1, 2, 3, 4, 5, 6]}

        def getbank():
            a = pstate["avail"]
            i = a[pstate["next"] % len(a)]
            pstate["next"] += 1
            return pbank[i], b_pbank[i]

        ring_state = {"next": 0}
        x_ops = []

        def wload(src_ap, shape3):
            i = ring_state["next"] % NSLOT
            ring_state["next"] += 1
            a, b = shape3
            view = ring[i][:, 0:a * b].rearrange("p (a b) -> p a b", a=a)
            ex = ()
            if ring_state["next"] == 1:
                ex = tuple(x_ops)
            K.op("pool", lambda h: h.dma_start(out=view, in_=src_ap), writes=[b_ring[i]], dma=True, name="wload", extra=ex)
            return view, b_ring[i]

        def wcols(w2d, col0, ncol=256):
            src = w2d.rearrange("(kc p) n -> p kc n", p=128)[:, :, col0:col0 + ncol]
            return wload(src, (8, ncol))

        def wrows(w2d, row0):
            src = w2d[row0:row0 + 256, :].rearrange("(r p) n -> p r n", p=128)
            return wload(src, (2, 1024))

        for (dst, src, nm) in ((ident, ident_d, "ident"), (ones, ones_d, "ones"), (cols, cols_d, "cols"),
                               (maskT, maskT_d, "maskT"), (smask, smask_d, "smask"), (kmcol, kmcol_d, "kmcol")):
            K.op("sp" if nm == "ident" else "act", lambda h, dst=dst, src=src: h.dma_start(out=dst[:], in_=src),
                 writes=[b_const], dma=True, name=nm)
        K.op("dve", lambda h: h.tensor_copy(out=identb[:], in_=ident[:]), reads=[b_const], writes=[b_const])
        EPS_R = None

        def gcol(gi, c):
            return cols[:, gi * 8 + c:gi * 8 + c + 1]

        def wccol(j, c):
            return cols[:, 40 + j * 8 + c:40 + j * 8 + c + 1]

        def gncol(m):
            return cols[:, 64 + m:64 + m + 1]

        epsc = T(top, "epsc", [128, 2], F32)
        K.op("dve", lambda h: h.memset(epsc[:, 0:1], RMS_EPS), writes=[b_const])
        K.op("dve", lambda h: h.memset(epsc[:, 1:2], GN_EPS), reads=[b_const], writes=[b_const])

        def mm_group(out_ap, pairs, reads, bbuf, name="mm"):
            n = len(pairs)

            def fn(h):
                ins = None
                for i, (l, r) in enumerate(pairs):
                    ins = h.matmul(out_ap, lhsT=l, rhs=r, start=(i == 0), stop=(i == n - 1))
                return ins
            return K.op("pe", fn, reads=reads, writes=[bbuf], name=name)

        nstate = {"n": 0}
        pending_h = {}

        def need_h(tt):
            f = pending_h.pop(tt, None)
            if f is not None:
                f()


        def norm_tile(gi, tt, final_cb=None, defer=False):
            t0, N = TT[tt]
            p = nstate["n"] % 2
            nstate["n"] += 1
            acc_, rs_, bacc_, brs_ = acc2[p], rs2[p], b_acc2[p], b_rs2[p]
            for c in range(8):
                if c == 0:
                    K.op("act", lambda h: h.activation(out=acc_[:, :N], in_=xT[:, 0, t0:t0 + N], func=AF.Square),
                         reads=[b_x[0][tt]], writes=[bacc_], name="sq0")
                else:
                    s_ = c % 2
                    K.op("act", lambda h, c=c, s_=s_: h.activation(out=sq[s_][:, :N], in_=xT[:, c, t0:t0 + N], func=AF.Square),
                         reads=[b_x[c][tt]], writes=[b_sq[s_]], name="sq")
                    K.op("dve", lambda h, s_=s_: h.tensor_tensor(out=acc_[:, :N], in0=acc_[:, :N], in1=sq[s_][:, :N], op=ALU.add),
                         reads=[b_sq[s_], bacc_], writes=[bacc_], name="sqadd")
            def part2():
                return norm_part2(gi, tt, t0, N, acc_, rs_, bacc_, brs_, final_cb)
            if defer:
                return part2
            r = part2()
            for f in (r or []):
                f()

        def norm_part2(gi, tt, t0, N, acc_, rs_, bacc_, brs_, final_cb):
            bank, bb = getbank()
            K.op("pe", lambda h: h.matmul(bank[:, :N], lhsT=ones[:, :], rhs=acc_[:, :N], start=True, stop=True),
                 reads=[bacc_, b_const], writes=[bb], name="onesmm")
            K.op("act", lambda h: h.activation(out=rs_[:, :N], in_=bank[:, :N], func=AF.Ln, bias=epsc[:, 0:1], scale=1.0 / D),
                 reads=[bb, b_const], writes=[brs_], name="rln")
            K.op("act", lambda h: h.activation(out=rs_[:, :N], in_=rs_[:, :N], func=AF.Exp, scale=-0.5),
                 reads=[brs_], writes=[brs_], name="rexp")
            if final_cb is not None:
                return final_cb(tt, t0, N, rs_, brs_)
            for c in range(8):
                K.op("dve", lambda h, c=c: h.scalar_tensor_tensor(
                    out=hT[:, c, t0:t0 + N], in0=xT[:, c, t0:t0 + N], scalar=gcol(gi, c), in1=rs_[:, :N],
                    op0=ALU.mult, op1=ALU.mult),
                    reads=[b_x[c][tt], brs_, b_const], writes=[b_h[c][tt]], name="hnorm")

        def residual_add(bank, bb, mo, tt, t0, N, eng="dve"):
            K.op(eng, lambda h: h.tensor_tensor(out=xT[:, mo, t0:t0 + N], in0=xT[:, mo, t0:t0 + N], in1=bank[:, :N], op=ALU.add),
                 reads=[bb, b_x[mo][tt]], writes=[b_x[mo][tt]], name="resadd")

        with contextlib.ExitStack() as st:
            NX = 3
            xst2 = [T(st, f"xst{i}", [128, 2, D], F32) for i in range(NX)]
            b_xst2 = [Buf(f"xst{i}") for i in range(NX)]
            pend0 = [None]
            junk = T(st, "junk", [128, D], F32)
            b_junk = Buf("junk")
            ss = [T(st, f"ss{i}", [128, 4], F32) for i in range(2)]
            b_ss = [Buf("ss0"), Buf("ss1")]

            def norm0_block(tt, tb, rows, xs_, b_xs):
                j, p = tb % 4, tt % 2
                if rows < 128:
                    K.op("dve", lambda h: h.memset(acc2[p][64:128, 0:64], 0.0), reads=[b_acc2[p]], writes=[b_acc2[p]], name="acc0")
                K.op("act", lambda h: h.activation(out=junk[:rows, :], in_=xs_[:rows, :], func=AF.Square, accum_out=ss[p][:rows, j:j + 1]),
                     reads=[b_xs, b_ss[p]], writes=[b_ss[p], b_junk], name="sqacc")
                K.op("dve", lambda h: h.tensor_scalar(out=acc2[p][:rows, j * 128:j * 128 + rows], in0=ident[:rows, :rows],
                                                      scalar1=ss[p][:rows, j:j + 1], scalar2=None, op0=ALU.mult),
                     reads=[b_ss[p], b_const, b_acc2[p]], writes=[b_acc2[p]], name="ssdiag")

            for tb in range(17):
                rows = 128 if tb < 16 else 64
                tt = tb // 4
                s2 = (tb // 2) % NX
                xs_ = xst2[s2][:, tb % 2, :]
                b_xs = b_xst2[s2]
                if tb % 2 == 0:
                    if tb < 16:
                        src = xin[tb * 128:tb * 128 + 256, :].rearrange("(j p) d -> p j d", p=128)
                        x_ops.append(K.op("sp" if (tb // 2) % 2 == 0 else "pool", lambda h, s2=s2, src=src: h.dma_start(out=xst2[s2][:, :, :], in_=src),
                                          writes=[b_xs], dma=True, name="xload"))
                    else:
                        x_ops.append(K.op("sp", lambda h, s2=s2: h.dma_start(out=xst2[s2][:64, 0, :], in_=xin[2048:2112, :]),
                                          writes=[b_xs], dma=True, name="xload"))
                for half in range(2):
                    bank, bb = getbank()

                    def tr(h, bank=bank, xs_=xs_, rows=rows, half=half):
                        ins = None
                        for q in range(4):
                            c = half * 4 + q
                            ins = h.transpose(out=bank[:, q * 128:q * 128 + rows], in_=xs_[:rows, c * 128:(c + 1) * 128],
                                              identity=ident[:rows, :rows])
                        return ins
                    K.op("pe", tr, reads=[b_xs, b_const], writes=[bb], name="xtr")
                    eng = "act" if half == 0 else "dve"

                    def ev(h, bank=bank, tb=tb, rows=rows, half=half, eng=eng):
                        src = bank[:, :].rearrange("p (q t) -> p q t", q=4)[:, :, :rows]
                        dst = xT[:, half * 4:half * 4 + 4, tb * 128:tb * 128 + rows]
                        if eng == "act":
                            return h.copy(out=dst, in_=src)
                        return h.tensor_copy(out=dst, in_=src)
                    K.op(eng, ev, reads=[bb], writes=[b_x[half * 4 + q][tt] for q in range(4)], name="xev")
                norm0_block(tt, tb, rows, xs_, b_xs)
                if tb % 4 == 3 or tb == 16:
                    if pend0[0] is not None:
                        pend0[0]()
                        pend0[0] = None
                    t0_, N_ = TT[tt]
                    p_ = tt % 2
                    p2 = (lambda tt=tt, t0_=t0_, N_=N_, p_=p_: norm_part2(0, tt, t0_, N_, acc2[p_], rs2[p_], b_acc2[p_], b_rs2[p_], None))
                    if tt >= 3:
                        pending_h[tt] = p2
                    else:
                        pend0[0] = p2
            nstate["n"] = 5
        K.fence()

        def conv_stage():
            with contextlib.ExitStack() as st:
                byT = T(st, "byT", [128, 8, NT], BF16)
                b_by = [[Buf() for _ in range(5)] for _ in range(8)]
                uf = [T(st, "uf0", [128, 2050], F32)] * 2
                us = [T(st, "us0", [128, 16, 6], F32)] * 2
                b_u = [[Buf() for _ in range(6)]] * 2
                tcp = [T(st, "tcp0", [128, 512], F32)] * 2
                b_tcp = [Buf()] * 2
                yt = [T(st, "yt0", [128, 512], F32)] * 2
                b_yt = [Buf()] * 2
                cst = T(st, "cst", [128, 8, 32], F32)
                cso = T(st, "cso", [128, 8, 34], F32)
                csr = T(st, "csr", [34, D], F32)
                scs = csr
                b_cst, b_cso, b_csr = Buf(), Buf(), Buf()
                b_scs = b_csr

                K.op("sp", lambda h: h.dma_start(out=scs[:32, :], in_=sconv), writes=[b_scs], dma=True)
                bank, bb = getbank()

                def trs(h, bank=bank):
                    ins = None
                    for c in range(8):
                        ins = h.transpose(out=bank[:, c * 32:(c + 1) * 32], in_=scs[:32, c * 128:(c + 1) * 128], identity=ident[:32, :32])
                    return ins
                K.op("pe", trs, reads=[b_scs, b_const], writes=[bb])
                K.op("act", lambda h, bank=bank: h.copy(out=cst[:, :, :], in_=bank[:, 0:256].rearrange("p (c s) -> p c s", c=8)),
                     reads=[bb], writes=[b_cst])
                cnt = 0
                for mp in range(4):
                    wc_, bwc = wcols(conv_w_in, D + mp * 256)
                    wh_, bwh = wcols(conv_w_in, 2 * D + mp * 256)
                    wb_, bwb = wcols(conv_w_in, mp * 256)
                    for q in range(2):
                        m = 2 * mp + q
                        ui = m % 2
                        ufm, usm, bu = uf[ui], us[ui], b_u[ui]
                        K.op("dve", lambda h, ufm=ufm: h.memset(ufm[:, 0:2], 0.0), writes=[bu[5]])
                        K.op("dve", lambda h, usm=usm, m=m: h.tensor_copy(out=usm[:, :, 0:2], in_=cst[:, m, :].rearrange("p (s j) -> p s j", j=2)),
                             reads=[b_cst], writes=[bu[4]])
                        for tt, (t0, N) in enumerate(TT):
                            need_h(tt)
                            sl = slice(q * 128, (q + 1) * 128)
                            bc, bbc = getbank()
                            mm_group(bc[:, :N], [(wc_[:, kc, sl], hT[:, kc, t0:t0 + N]) for kc in range(8)],
                                     [bwc] + [b_h[kc][tt] for kc in range(8)], bbc)
                            bh, bbh = getbank()
                            mm_group(bh[:, :N], [(wh_[:, kc, sl], hT[:, kc, t0:t0 + N]) for kc in range(8)],
                                     [bwh] + [b_h[kc][tt] for kc in range(8)], bbh)
                            bbk, bbb = getbank()
                            mm_group(bbk[:, :N], [(wb_[:, kc, sl], hT[:, kc, t0:t0 + N]) for kc in range(8)],
                                     [bwb] + [b_h[kc][tt] for kc in range(8)], bbb)
                            ti = cnt % 2
                            cnt += 1
                            K.op("act", lambda h, bc=bc, N=N, ti=ti: h.copy(out=tcp[ti][:, :N], in_=bc[:, :N]),
                                 reads=[bbc], writes=[b_tcp[ti]])
                            if tt < 4:
                                udst = ufm[:, 2 + t0:2 + t0 + N]
                                K.op("dve", lambda h, udst=udst, bh=bh, N=N, ti=ti: h.tensor_tensor(out=udst, in0=tcp[ti][:, :N], in1=bh[:, :N], op=ALU.mult),
                                     reads=[b_tcp[ti], bbh], writes=[bu[tt]])
                                u0, u1, u2 = ufm[:, t0:t0 + N], ufm[:, t0 + 1:t0 + 1 + N], ufm[:, t0 + 2:t0 + 2 + N]
                                ydst = yt[ti][:, :N]
                                rd = [bu[tt], bu[tt - 1] if tt > 0 else bu[5], b_const]
                            else:
                                udst = usm[:, :, 2:6]
                                K.op("dve", lambda h, udst=udst, bh=bh, ti=ti: h.tensor_tensor(
                                    out=udst, in0=tcp[ti][:, :64].rearrange("p (s t) -> p s t", t=4),
                                    in1=bh[:, :64].rearrange("p (s t) -> p s t", t=4), op=ALU.mult),
                                    reads=[b_tcp[ti], bbh, bu[4]], writes=[bu[4]])
                                u0, u1, u2 = usm[:, :, 0:4], usm[:, :, 1:5], usm[:, :, 2:6]
                                ydst = yt[ti][:, :64].rearrange("p (s t) -> p s t", t=4)
                                rd = [bu[4], b_const]
                            K.op("dve", lambda h, ydst=ydst, u2=u2, m=m: h.tensor_scalar(out=ydst, in0=u2, scalar1=wccol(2, m), scalar2=None, op0=ALU.mult),
                                 reads=rd, writes=[b_yt[ti]])
                            K.op("dve", lambda h, ydst=ydst, u1=u1, m=m: h.scalar_tensor_tensor(out=ydst, in0=u1, scalar=wccol(1, m), in1=ydst, op0=ALU.mult, op1=ALU.add),
                                 reads=rd + [b_yt[ti]], writes=[b_yt[ti]])
                            K.op("dve", lambda h, ydst=ydst, u0=u0, m=m: h.scalar_tensor_tensor(out=ydst, in0=u0, scalar=wccol(0, m), in1=ydst, op0=ALU.mult, op1=ALU.add),
                                 reads=rd + [b_yt[ti]], writes=[b_yt[ti]])
                            K.op("dve", lambda h, bbk=bbk, ti=ti, m=m, t0=t0, N=N: h.tensor_tensor(out=byT[:, m, t0:t0 + N], in0=yt[ti][:, :N], in1=bbk[:, :N], op=ALU.mult),
                                 reads=[b_yt[ti], bbb], writes=[b_by[m][tt]])
                        K.op("act", lambda h, ufm=ufm, m=m: h.copy(out=cso[:, m, 0:2], in_=ufm[:, 2048:2050]),
                             reads=[bu[3]], writes=[b_cso])
                        K.op("act", lambda h, usm=usm, m=m: h.copy(out=cso[:, m, 2:34].rearrange("p (s j) -> p s j", j=2), in_=usm[:, :, 4:6]),
                             reads=[bu[4]], writes=[b_cso])
                for half in range(2):
                    bank, bb = getbank()

                    def trc(h, bank=bank, half=half):
                        ins = None
                        for q in range(4):
                            c = half * 4 + q
                            ins = h.transpose(out=bank[:34, q * 128:(q + 1) * 128], in_=cso[:, c, :], identity=ident[:, :])
                        return ins
                    K.op("pe", trc, reads=[b_cso, b_const], writes=[bb])
                    K.op("act", lambda h, bank=bank, half=half: h.copy(out=csr[:34, half * 512:(half + 1) * 512], in_=bank[:34, :]),
                         reads=[bb], writes=[b_csr])
                K.op("sp", lambda h: h.dma_start(out=convo_d, in_=csr[:34, :]), reads=[b_csr], dma=True, name="convo")
                wos = [wcols(conv_w_out, mp * 256) for mp in range(4)]
                pendc = [None]
                for tt, (t0, N) in enumerate(TT):
                    for mo in range(8):
                        wo_, bwo = wos[mo // 2]
                        sl = slice((mo % 2) * 128, (mo % 2) * 128 + 128)
                        bank, bb = getbank()
                        mm_group(bank[:, :N], [(wo_[:, kc, sl], byT[:, kc, t0:t0 + N]) for kc in range(8)],
                                 [bwo] + [b_by[kc][tt] for kc in range(8)], bb)
                        residual_add(bank, bb, mo, tt, t0, N)
                        if mo == 3 and pendc[0] is not None:
                            pendc[0]()
                            pendc[0] = None
                    pendc[0] = norm_tile(1, tt, defer=True)
                pending_h[4] = pendc[0]
                K.fence()

        def ffn_stage(li):
            wg2, wu2, wd2 = ffn_w_gate[li], ffn_w_up[li], ffn_w_down[li]
            with contextlib.ExitStack() as st:
                actT = T(st, "actT", [128, 8, NT], BF16)
                b_a = [[Buf() for _ in range(5)] for _ in range(8)]
                sl_ = [T(st, f"silu{i}", [128, 512], F32) for i in range(2)]
                b_sl = [Buf(), Buf()]
                fcb = make_final_cb(st) if li == 1 else None
                cnt = 0
                for (f0, G) in ((0, 6), (6, 8), (14, 8)):
                    for fp in range(G // 2):
                        f = f0 + 2 * fp
                        wg_, bwg = wcols(wg2, f * 128)
                        wu_, bwu = wcols(wu2, f * 128)
                        for q in range(2):
                            fl = 2 * fp + q
                            sl = slice(q * 128, (q + 1) * 128)
                            for tt, (t0, N) in enumerate(TT):
                                need_h(tt)
                                bg, bbg = getbank()
                                mm_group(bg[:, :N], [(wg_[:, kc, sl], hT[:, kc, t0:t0 + N]) for kc in range(8)],
                                         [bwg] + [b_h[kc][tt] for kc in range(8)], bbg)
                                bu_, bbu = getbank()
                                mm_group(bu_[:, :N], [(wu_[:, kc, sl], hT[:, kc, t0:t0 + N]) for kc in range(8)],
                                         [bwu] + [b_h[kc][tt] for kc in range(8)], bbu)
                                ti = cnt % 2
                                cnt += 1
                                K.op("act", lambda h, bg=bg, N=N, ti=ti: h.activation(out=sl_[ti][:, :N], in_=bg[:, :N], func=AF.Silu),
                                     reads=[bbg], writes=[b_sl[ti]], name="silu")
                                K.op("dve", lambda h, bu_=bu_, N=N, ti=ti, fl=fl, t0=t0: h.tensor_tensor(
                                    out=actT[:, fl, t0:t0 + N], in0=sl_[ti][:, :N], in1=bu_[:, :N], op=ALU.mult),
                                    reads=[b_sl[ti], bbu], writes=[b_a[fl][tt]], name="actmul")
                                bgq.tick()
                    wds = [wrows(wd2, (f0 + 2 * j) * 128) for j in range(G // 2)]
                    lastg = (f0 == 14)

                    def down(mo, tt):
                        t0, N = TT[tt]
                        bank, bb = getbank()
                        pairs = [(wds[fl // 2][0][:, fl % 2, mo * 128:(mo + 1) * 128], actT[:, fl, t0:t0 + N]) for fl in range(G)]
                        mm_group(bank[:, :N], pairs, [w[1] for w in wds] + [b_a[fl][tt] for fl in range(G)], bb)
                        residual_add(bank, bb, mo, tt, t0, N)
                    if not lastg:
                        for mo in range(8):
                            for tt in range(5):
                                down(mo, tt)
                    else:
                        pendf = None
                        steps = []
                        for tt in range(5):
                            for mo in range(8):
                                down(mo, tt)
                                if li == 0:
                                    if mo == 3 and pendf is not None:
                                        pendf()
                                        pendf = None
                                else:
                                    if mo == 0 and pendf is not None:
                                        steps = list(pendf() or [])
                                        pendf = None
                                    elif mo in (1, 3, 4, 6) and steps:
                                        steps.pop(0)()
                            while steps:
                                steps.pop(0)()
                            if li == 0:
                                pendf = norm_tile(2, tt, defer=True)
                            else:
                                pendf = norm_tile(4, tt, final_cb=fcb, defer=True)
                        if li == 0:
                            pending_h[4] = pendf
                        else:
                            for f in (pendf() or []):
                                f()
                bgq.drain()
                if li == 0:
                    K.fence()
                else:
                    K.barrier()

        def ret_stage():
            wq_all = ret_w_in
            HT = 1088
            halves = ((0, (0, 1)), (1024, (2, 3, 4)))
            with contextlib.ExitStack() as st:
                QK = T(st, "QK", [128, 4, HT], BF16)
                b_qk = [[Buf() for _ in range(9)] for _ in range(4)]
                ktok = T(st, "ktok", [128, 9, 256], BF16)
                b_kt = [Buf() for _ in range(9)]
                vtok = T(st, "vtok", [128, 9, 512], BF16)
                b_vt = [Buf() for _ in range(9)]
                tmp = [T(st, f"rtmp{i}", [128, 512], F32) for i in range(5)]
                b_tmp = [Buf() for _ in range(5)]
                tab = [T(st, f"tab{i}", [128, 512], F32) for i in range(2)]
                b_tab = [Buf() for _ in range(2)]
                PT = [T(st, f"PT{i}", [128, 128], BF16) for i in range(2)]
                b_PT = [Buf(), Buf()]
                PTs = T(st, "PTs", [64, 64], BF16)
                b_PTs = Buf()
                on = [T(st, f"on{i}", [128, 512], BF16) for i in range(2)]
                b_on = [Buf(), Buf()]
                stats = [T(st, f"stats{i}", [128, 8], F32) for i in range(2)]
                mv = [T(st, f"mv{i}", [128, 8], F32) for i in range(2)]
                b_stats, b_mv = [Buf(), Buf()], [Buf(), Buf()]
                Sb = T(st, "Sb", [128, 2, 512], BF16)
                b_Sb = Buf()
                S0b = [T(st, f"S0b{i}", [128, 2, 512], BF16) for i in range(2)]
                b_S0b = [Buf(), Buf()]
                S0b += [tab[j][:, :].bitcast(BF16).rearrange("p (a b) -> p a b", a=2) for j in range(2)]
                b_S0b += [b_tab[0], b_tab[1]]
                qpad = T(st, "qpad", [128, 2, 1024], BF16)
                b_qpad = Buf()
                gt = [tmp[3], tmp[4]]
                b_gt = [b_tmp[3], b_tmp[4]]

                K.op("dve", lambda h: h.memset(qpad[:, :, :], 0.0), writes=[b_qpad])
                Sf = T(st, "Sf", [128, 2, 512], F32)
                b_Sf = [Buf(), Buf()]

                cnt = {"tmp": 0, "on": 0, "gt": 0, "s0": 0, "pt": 0, "s0b": 0}
                pendr = [None]
                def head_half(hd, hi, tok0, tiles):
                    if True:
                        nblk = 8 if hi == 0 else 9
                        for which in (1, 0):
                            w_, bw = wcols(wq_all, which * D + hd * 256)
                            rope_d = ropeq_d if which == 0 else ropek_d
                            for tt in tiles:
                                need_h(tt)
                                t0, N = TT[tt]
                                l0 = t0 - tok0
                                lb = [l0 // 128 + i for i in range((N + 127) // 128)]
                                for j in range(2):
                                    K.op("sp", lambda h, j=j, t0=t0, N=N, rope_d=rope_d, which=which: h.dma_start(
                                        out=tab[j][:, :N], in_=rope_d[hd, j, :, t0:t0 + N]),
                                        writes=[b_tab[j]], dma=True, name="tab")
                                ct, sn = tab[0], tab[1]
                                bct, bsn = b_tab[0], b_tab[1]
                                b1, bb1 = getbank()
                                mm_group(b1[:, :N], [(w_[:, kc, 0:128], hT[:, kc, t0:t0 + N]) for kc in range(8)],
                                         [bw] + [b_h[kc][tt] for kc in range(8)], bb1)
                                b2, bb2 = getbank()
                                mm_group(b2[:, :N], [(w_[:, kc, 128:256], hT[:, kc, t0:t0 + N]) for kc in range(8)],
                                         [bw] + [b_h[kc][tt] for kc in range(8)], bb2)
                                t2, ta, tb_ = tmp[0], tmp[1], tmp[2]
                                K.op("act", lambda h, b2=b2, N=N: h.copy(out=tmp[0][:, :N], in_=b2[:, :N]), reads=[bb2], writes=[b_tmp[0]])
                                K.op("dve", lambda h, b1=b1, N=N, ct=ct: h.tensor_tensor(out=tmp[1][:, :N], in0=b1[:, :N], in1=ct[:, :N], op=ALU.mult),
                                     reads=[bb1, bct], writes=[b_tmp[1]])
                                K.op("dve", lambda h, N=N, sn=sn: h.tensor_tensor(out=tmp[2][:, :N], in0=tmp[0][:, :N], in1=sn[:, :N], op=ALU.mult),
                                     reads=[b_tmp[0], bsn], writes=[b_tmp[2]])
                                K.op("dve", lambda h, N=N, l0=l0, which=which: h.tensor_tensor(out=QK[:, 2 * which, l0:l0 + N], in0=tmp[1][:, :N], in1=tmp[2][:, :N], op=ALU.subtract),
                                     reads=[b_tmp[1], b_tmp[2]], writes=[b_qk[2 * which][b] for b in lb])
                                K.op("dve", lambda h, b1=b1, N=N, sn=sn: h.tensor_tensor(out=tmp[3][:, :N], in0=b1[:, :N], in1=sn[:, :N], op=ALU.mult),
                                     reads=[bb1, bsn], writes=[b_tmp[3]])
                                K.op("dve", lambda h, N=N, ct=ct: h.tensor_tensor(out=tmp[4][:, :N], in0=tmp[0][:, :N], in1=ct[:, :N], op=ALU.mult),
                                     reads=[b_tmp[0], bct], writes=[b_tmp[4]])
                                K.op("dve", lambda h, N=N, l0=l0, which=which: h.tensor_tensor(out=QK[:, 2 * which + 1, l0:l0 + N], in0=tmp[3][:, :N], in1=tmp[4][:, :N], op=ALU.add),
                                     reads=[b_tmp[3], b_tmp[4]], writes=[b_qk[2 * which + 1][b] for b in lb])
                                bgq.tick()
                        wv = [wcols(wq_all, 2 * D + hd * 512 + j * 256) for j in range(2)]
                        for lbk in range(nblk):
                            rows = 64 if (hi == 1 and lbk == 8) else 128
                            g0 = tok0 + lbk * 128
                            tt = g0 // 512
                            bank, bb = getbank()
                            for j in range(2):
                                mm_group(bank[:rows, j * 256:(j + 1) * 256],
                                         [(hT[:, kc, g0:g0 + rows], wv[j][0][:, kc, :]) for kc in range(8)],
                                         [wv[j][1]] + [b_h[kc][tt] for kc in range(8)], bb, name="vproj")
                            K.op("act", lambda h, bank=bank, lbk=lbk, rows=rows: h.copy(out=vtok[:rows, lbk, :], in_=bank[:rows, :]),
                                 reads=[bb], writes=[b_vt[lbk]], name="vtok")
                            bgq.tick()
                        groups = [list(range(0, 4)), list(range(4, 8))] + ([[8]] if hi == 1 else [])
                        for grp in groups:
                            rows = 64 if grp == [8] else 128
                            ptbs["n"] += 1

                            def trk(h, grp=grp, rows=rows):
                                ins = None
                                for i, lbk in enumerate(grp):
                                    for kc in range(2):
                                        ins = h.transpose(out=ptb[:rows, i * 256 + kc * 128:i * 256 + (kc + 1) * 128],
                                                          in_=QK[:, 2 + kc, lbk * 128:lbk * 128 + rows], identity=identb[:, :])
                                return ins
                            K.op("pe", trk, reads=[b_qk[2][l] for l in grp] + [b_qk[3][l] for l in grp] + [b_const], writes=[b_ptb[0]], name="trk")
                            n = len(grp)
                            K.op("act", lambda h, grp=grp, rows=rows, n=n: h.copy(out=ktok[:rows, grp[0]:grp[0] + n, :],
                                                                                 in_=ptb[:rows, 0:n * 256].rearrange("p (b d) -> p b d", b=n)),
                                 reads=[b_ptb[0]], writes=[b_kt[l] for l in grp], name="ktok")
                        def chunk(lbk, rows=128, bo_=None):
                            c0 = lbk * 128
                            first = (hi == 0 and lbk == 0)
                            last = (hi == 1 and lbk == 7)
                            blk = slice(c0, c0 + rows)
                            pi = cnt["pt"] % 2
                            cnt["pt"] += 1
                            if bo_ is None:
                                bo, bbo = pbank[4 + pi], b_pbank[4 + pi]
                            else:
                                bo, bbo = bo_
                            kvb = []

                            def A1():
                                bs, bbs = getbank()
                                mm_group(bs[:rows, :rows], [(QK[:, 2 + kc, blk], QK[:, kc, blk]) for kc in range(2)],
                                         [b_qk[a_][lbk] for a_ in range(4)], bbs, name="scoresT")
                                K.op("dve", lambda h: h.tensor_tensor(out=PT[pi][:rows, :rows], in0=bs[:rows, :rows], in1=maskT[:rows, :rows], op=ALU.mult),
                                     reads=[bbs, b_const], writes=[b_PT[pi]], name="PT")

                            def A2():
                                pairs = [(PT[pi][:rows, :rows], vtok[:rows, lbk, :])]
                                rd = [b_PT[pi], b_vt[lbk]]
                                if not first:
                                    pairs += [(QK[:, kc, blk], Sb[:, kc, :]) for kc in range(2)]
                                    rd += [b_Sb, b_qk[0][lbk], b_qk[1][lbk]]
                                mm_group(bo[:rows, :], pairs, rd, bbo, name="o")

                            def AK():
                                for kc in range(2):
                                    bk, bbk = getbank()
                                    kvb.append((bk, bbk))
                                    K.op("pe", lambda h, bk=bk, kc=kc: h.matmul(
                                        bk[:, :], lhsT=ktok[:, lbk, kc * 128:(kc + 1) * 128], rhs=vtok[:, lbk, :], start=True, stop=True),
                                        reads=[b_kt[lbk], b_vt[lbk]], writes=[bbk], name="kv")

                            def A3():
                                for kc in range(2):
                                    bk, bbk = kvb[kc]
                                    if first:
                                        K.op("dve", lambda h, bk=bk, kc=kc: h.tensor_copy(out=Sf[:, kc, :], in_=bk[:, :]),
                                             reads=[bbk], writes=[b_Sf[kc]], name="Sinit")
                                    else:
                                        K.op("dve", lambda h, bk=bk, kc=kc: h.tensor_tensor(out=Sf[:, kc, :], in0=Sf[:, kc, :], in1=bk[:, :], op=ALU.add),
                                             reads=[bbk, b_Sf[kc]], writes=[b_Sf[kc]], name="Sacc")

                            def A4():
                                if not last:
                                    K.op("act", lambda h: h.copy(out=Sb[:, :, :], in_=Sf[:, :, :]), reads=[b_Sf[0], b_Sf[1]], writes=[b_Sb], name="Sb")
                                else:
                                    for kc in range(2):
                                        K.op("act", lambda h, kc=kc: h.activation(out=Sf[:, kc, :], in_=Sf[:, kc, :], func=AF.Copy, scale=g2048[hd]),
                                             reads=[b_Sf[kc]], writes=[b_Sf[kc]], name="retp")
                                    K.op("sp", lambda h: h.dma_start(out=retp_d[hd].rearrange("(kc p) v -> p kc v", p=128), in_=Sf[:, :, :]),
                                         reads=[b_Sf[0], b_Sf[1]], dma=True, name="retp_out")

                            oi = cnt["on"] % 2
                            cnt["on"] += 1
                            st_, mv_, bst_, bmv_ = stats[oi], mv[oi], b_stats[oi], b_mv[oi]

                            def B1():
                                K.op("dve", lambda h: h.bn_stats(out=st_[:rows, 0:6], in_=bo[:rows, :]), reads=[bbo], writes=[bst_], name="bnst")
                                K.op("dve", lambda h: h.bn_aggr(out=mv_[:rows, 0:2], in_=st_[:rows, 0:6]), reads=[bst_, bmv_], writes=[bmv_], name="bnag")
                                K.op("act", lambda h: h.activation(out=mv_[:rows, 2:3], in_=mv_[:rows, 1:2], func=AF.Sqrt, bias=epsc[:rows, 1:2], scale=1.0),
                                     reads=[bmv_, b_const], writes=[bmv_], name="gnsqrt")

                            def B2():
                                K.op("dve", lambda h: h.reciprocal(out=mv_[:rows, 3:4], in_=mv_[:rows, 2:3]), reads=[bmv_], writes=[bmv_], name="gnrec")
                                K.op("dve", lambda h: h.tensor_scalar(out=mv_[:rows, 4:5], in0=mv_[:rows, 0:1], scalar1=mv_[:rows, 3:4], scalar2=-1.0,
                                                                      op0=ALU.mult, op1=ALU.mult),
                                     reads=[bmv_], writes=[bmv_], name="gnnb")
                                K.op("act", lambda h: h.activation(out=on[oi][:rows, :], in_=bo[:rows, :], func=AF.Identity, bias=mv_[:rows, 4:5], scale=mv_[:rows, 3:4]),
                                     reads=[bbo, bmv_], writes=[b_on[oi]], name="gnorm")

                            ph = ptbs["n"] % 2
                            ptbs["n"] += 1

                            def B3():
                                def tro(h):
                                    ins = None
                                    for m in range(4):
                                        ins = h.transpose(out=ptb[:, ph * 512 + m * 128:ph * 512 + m * 128 + rows], in_=on[oi][:rows, m * 128:(m + 1) * 128],
                                                          identity=identb[:rows, :rows])
                                    return ins
                                K.op("pe", tro, reads=[b_on[oi], b_const], writes=[b_ptb[ph]], name="tro")

                            def B4():
                                K.op("act", lambda h: h.copy(out=QK[:, :, c0:c0 + rows],
                                                             in_=ptb[:, ph * 512:ph * 512 + 512].rearrange("p (m t) -> p m t", m=4)[:, :, :rows]),
                                     reads=[b_ptb[ph]], writes=[b_qk[a_][lbk] for a_ in range(4)], name="oT")
                            return dict(A1=A1, AK=AK, A2=A2, A3=A3, A4=A4, B1=B1, B2=B2, B3=B3, B4=B4)

                        def state_load(s, par):
                            si = s % NS0
                            km, bkm = kmk[s % 2], b_kmk[s % 2]
                            K.op("sp", lambda h: h.dma_start(out=sS0f[si][:, :, :], in_=sret[s, hd].rearrange("(kc p) v -> p kc v", p=128)),
                                 writes=[b_sS0f[si]], dma=True, name="s0f")
                            K.op("dve", lambda h: h.tensor_scalar(out=km[:64, :], in0=sk[par][:64, :], scalar1=kmcol[:, hd * 16 + s:hd * 16 + s + 1],
                                                                  scalar2=None, op0=ALU.mult),
                                 reads=[b_skv[par], b_const], writes=[bkm], name="kmask")

                        def state_comp(s, par):
                            si = s % NS0
                            km, bkm = kmk[s % 2], b_kmk[s % 2]
                            for kc in range(2):
                                bk, bbk = getbank()
                                K.op("pe", lambda h, bk=bk, kc=kc: h.matmul(bk[:, :], lhsT=km[:64, kc * 128:(kc + 1) * 128], rhs=sv[par][:64, :], start=True, stop=True),
                                     reads=[bkm, b_skv[par]], writes=[bbk], name="kv_s")
                                K.op("dve", lambda h, bk=bk, kc=kc: h.scalar_tensor_tensor(
                                    out=sS0f[si][:, kc, :], in0=sS0f[si][:, kc, :], scalar=g4[hd], in1=bk[:, :], op0=ALU.mult, op1=ALU.add),
                                    reads=[bbk, b_sS0f[si]], writes=[b_sS0f[si]], name="snew")
                            K.op("sp", lambda h: h.dma_start(out=rets_d[s, hd].rearrange("(kc p) v -> p kc v", p=128), in_=sS0f[si][:, :, :]),
                                 reads=[b_sS0f[si]], dma=True, name="rets_out")

                        def s0b_load(s):
                            si = s % 4
                            K.op("pool", lambda h: h.dma_start(out=S0b[si][:, :, :], in_=sret[s, hd].rearrange("(kc p) v -> p kc v", p=128)),
                                 writes=[b_S0b[si]], dma=True, name="s0b")

                        def cross_seq(s, bo6):
                            si = s % 4

                            def cross(h):
                                ins = None
                                for kc in range(2):
                                    ins = h.matmul(bo6[:64, :], lhsT=qpad[:, kc, s * 64:(s + 1) * 64], rhs=S0b[si][:, kc, :], start=False, stop=(s == 15 and kc == 1))
                                return ins
                            K.op("pe", cross, reads=[b_qpad, b_S0b[si]], writes=[b_pbank[6]], name="cross_s")

                        pstate["avail"] = [0, 1, 2, 3]
                        if hi == 1:
                            par = hd % 2
                            bo6 = pbank[6]
                            sblk = slice(1024, 1088)
                            bs, bbs = getbank()
                            mm_group(bs[:64, :64], [(QK[:, 2 + kc, sblk], QK[:, kc, sblk]) for kc in range(2)],
                                     [b_qk[a_][8] for a_ in range(4)], bbs, name="scoresT_s")
                            K.op("dve", lambda h: h.tensor_tensor(out=PTs[:64, :64], in0=bs[:64, :64], in1=smask[:64, :64], op=ALU.mult),
                                 reads=[bbs, b_const], writes=[b_PTs], name="PTs")
                            K.op("pe", lambda h: h.matmul(bo6[:64, :], lhsT=PTs[:64, :64], rhs=vtok[:64, 8, :], start=True, stop=False),
                                 reads=[b_PTs, b_vt[8]], writes=[b_pbank[6]], name="o_s_inner")
                            for kc in range(2):
                                K.op("dve", lambda h, kc=kc: h.tensor_copy(
                                    out=qpad[:, kc, 0:1020].rearrange("p (s e) -> p s e", e=68)[:, :, 0:4],
                                    in_=QK[:, kc, 1024:1084].rearrange("p (s t) -> p s t", t=4)),
                                    reads=[b_qk[kc][8], b_qpad], writes=[b_qpad], name="qpad")
                                K.op("dve", lambda h, kc=kc: h.tensor_copy(out=qpad[:, kc, 1020:1024], in_=QK[:, kc, 1084:1088]),
                                     reads=[b_qk[kc][8], b_qpad], writes=[b_qpad], name="qpad2")
                            K.op("act", lambda h: h.copy(out=sk[par][:64, :], in_=ktok[:64, 8, :]), reads=[b_kt[8]], writes=[b_skv[par]], name="skcp")
                            K.op("act", lambda h: h.copy(out=sv[par][:64, :], in_=vtok[:64, 8, :]), reads=[b_vt[8], b_skv[par]], writes=[b_skv[par]], name="svcp")
                            bgq.add(lambda: state_load(0, par))
                            for s_ in range(16):
                                if s_ + 1 < 16:
                                    bgq.add(lambda s_=s_: state_load(s_ + 1, par))
                                bgq.add(lambda s_=s_: state_comp(s_, par))
                        chs = [chunk(lbk) for lbk in range(8)]
                        nop = lambda: None
                        chs[0]["A1"](); chs[0]["AK"](); chs[0]["A2"](); chs[0]["A3"](); chs[0]["A4"]()
                        if hi == 1:
                            for s_ in range(4):
                                s0b_load(s_)
                        for ci in range(8):
                            cur = chs[ci]
                            nxt = chs[ci + 1] if ci + 1 < 8 else dict(A1=nop, AK=nop, A2=nop, A3=nop, A4=nop)
                            cur["B1"]()
                            nxt["A1"]()
                            nxt["AK"]()
                            nxt["A2"]()
                            cur["B2"]()
                            nxt["A3"]()
                            nxt["A4"]()
                            cur["B3"]()
                            cur["B4"]()
                            if hi == 1:
                                cross_seq(2 * ci, bo6)
                                cross_seq(2 * ci + 1, bo6)
                                if ci < 6:
                                    s0b_load(2 * ci + 4)
                                    s0b_load(2 * ci + 5)
                            bgq.tick()
                        if hi == 1:
                            sc = chunk(8, rows=64, bo_=(bo6, b_pbank[6]))
                            sc["B1"](); sc["B2"](); sc["B3"](); sc["B4"]()
                        pstate["avail"] = [0, 1, 2, 3, 4, 5, 6]

                        wg = [wcols(wq_all, 4 * D + hd * 512 + j * 256) for j in range(2)]
                        for m in range(4):
                            w_, bw = wg[m // 2]
                            sl = slice((m % 2) * 128, (m % 2) * 128 + 128)
                            for tt in tiles:
                                t0, N = TT[tt]
                                l0 = t0 - tok0
                                lb = [l0 // 128 + i for i in range((N + 127) // 128)]
                                bank, bb = getbank()
                                mm_group(bank[:, :N], [(w_[:, kc, sl], hT[:, kc, t0:t0 + N]) for kc in range(8)],
                                         [bw] + [b_h[kc][tt] for kc in range(8)], bb, name="gate")
                                gi = cnt["gt"] % 2
                                cnt["gt"] += 1
                                K.op("act", lambda h, bank=bank, N=N, gi=gi: h.activation(out=gt[gi][:, :N], in_=bank[:, :N], func=AF.Silu),
                                     reads=[bb], writes=[b_gt[gi]], name="gsilu")
                                K.op("dve", lambda h, m=m, l0=l0, N=N, gi=gi: h.scalar_tensor_tensor(
                                    out=QK[:, m, l0:l0 + N], in0=QK[:, m, l0:l0 + N], scalar=gncol(hd * 4 + m), in1=gt[gi][:, :N], op0=ALU.mult, op1=ALU.mult),
                                    reads=[b_gt[gi], b_const] + [b_qk[m][b] for b in lb], writes=[b_qk[m][b] for b in lb], name="gating")
                                bgq.tick()
                        wo = [wrows(ret_w_out, hd * 512 + j * 256) for j in range(2)]
                        for tt in tiles:
                            for mo in range(8):
                                t0, N = TT[tt]
                                l0 = t0 - tok0
                                lb = [l0 // 128 + i for i in range((N + 127) // 128)]
                                bank, bb = getbank()
                                pairs = [(wo[m // 2][0][:, m % 2, mo * 128:(mo + 1) * 128], QK[:, m, l0:l0 + N]) for m in range(4)]
                                mm_group(bank[:, :N], pairs, [w[1] for w in wo] + [b_qk[m][b] for m in range(4) for b in lb], bb, name="wout")
                                residual_add(bank, bb, mo, tt, t0, N)
                                bgq.tick()
                            if hd == NH - 1:
                                if pendr[0] is not None:
                                    pendr[0]()
                                pendr[0] = norm_tile(3, tt, defer=True)
                        if hd == NH - 1:
                            if hi == 1:
                                pending_h[4] = pendr[0]
                            else:
                                pendr[0]()
                            pendr[0] = None
                for hd_ in range(NH):
                    for hi_, (tok0_, tiles_) in enumerate(halves):
                        head_half(hd_, hi_, tok0_, tiles_)
                pstate["avail"] = [0, 1, 2, 3, 4, 5, 6]
                pstate["next"] = 0
                K.fence()

        def make_final_cb(st):
            yT = T(st, "yT", [128, 8, 256], F32)
            b_y = [Buf() for _ in range(8)]
            yst = [T(st, f"yst{i}", [128, D], F32) for i in range(2)]
            b_yst = [Buf(), Buf()]
            state = {"n": 0}

            def cb(tt, t0, N, rs_, brs_):
                steps = []
                for h0 in range(0, N, 256):
                    n = min(256, N - h0)

                    def stepB(h0=h0, n=n):
                        for c in range(8):
                            K.op("dve", lambda h, c=c: h.scalar_tensor_tensor(out=yT[:, c, :n], in0=xT[:, c, t0 + h0:t0 + h0 + n], scalar=gcol(4, c),
                                                                              in1=rs_[:, h0:h0 + n], op0=ALU.mult, op1=ALU.mult),
                                 reads=[b_x[c][tt], brs_, b_const], writes=[b_y[c]], name="ynorm")

                    def stepC(h0=h0, n=n):
                        for i in range((n + 127) // 128):
                            rows = min(128, n - i * 128)
                            tb = (t0 + h0) // 128 + i
                            s_ = state["n"] % 2
                            state["n"] += 1
                            for half in range(2):
                                bank, bb = getbank()

                                def tr(h, bank=bank, i=i, rows=rows, half=half):
                                    ins = None
                                    for q in range(4):
                                        cc = half * 4 + q
                                        ins = h.transpose(out=bank[:rows, q * 128:(q + 1) * 128], in_=yT[:, cc, i * 128:i * 128 + rows], identity=ident[:, :])
                                    return ins
                                K.op("pe", tr, reads=[b_y[half * 4 + q] for q in range(4)] + [b_const], writes=[bb], name="ytr")
                                eng = "act" if half == 0 else "dve"

                                def ev(h, bank=bank, s_=s_, rows=rows, half=half, eng=eng):
                                    if eng == "act":
                                        return h.copy(out=yst[s_][:rows, half * 512:(half + 1) * 512], in_=bank[:rows, :])
                                    return h.tensor_copy(out=yst[s_][:rows, half * 512:(half + 1) * 512], in_=bank[:rows, :])
                                K.op(eng, ev, reads=[bb], writes=[b_yst[s_]], name="yev")
                            K.op("sp", lambda h, s_=s_, tb=tb, rows=rows: h.dma_start(out=y_d[tb * 128:tb * 128 + rows, :], in_=yst[s_][:rows, :]),
                                 reads=[b_yst[s_]], dma=True, name="yout")
                    steps += [stepB, stepC]
                return steps
            return cb

        def final_stage():
            with contextlib.ExitStack() as st:
                cb = make_final_cb(st)
                for tt in range(5):
                    norm_tile(4, tt, final_cb=cb)
                K.barrier()

        stages = [("conv", conv_stage), ("ffn0", lambda: ffn_stage(0)), ("ret", ret_stage), ("ffn1", lambda: ffn_stage(1))]
        if stop_after == "load":
            stages = []
        for nm, fnc in stages:
            fnc()
            if stop_after == nm:
                break
        if stop_after is not None:
            final_stage()
        K.emit()
    return nc, K


_CACHE = {}


def kernel(x_prompt, x_sample, state_conv, state_ret, norm_mix, norm_ffn, conv_w_in, conv_w,
           conv_w_out, ret_w_in, ret_gn, ret_w_out, ffn_w_gate, ffn_w_up, ffn_w_down, final_norm,
           _stop_after=None):
    f = lambda a: np.ascontiguousarray(np.asarray(a, dtype=np.float32))
    x_prompt, x_sample, state_conv, state_ret = f(x_prompt), f(x_sample), f(state_conv), f(state_ret)
    consts, g4, g2048 = _consts()
    key = ("nc", _stop_after)
    if key not in _CACHE:
        _CACHE[key] = build_nc(g4, g2048, stop_after=_stop_after)
    nc, _ = _CACHE[key]
    def colify(v):
        return f(v).reshape(-1, 128).T
    cols = np.concatenate([colify(norm_mix[0]), colify(norm_ffn[0]), colify(norm_mix[1]), colify(norm_ffn[1]),
                           colify(final_norm), colify(f(conv_w)[0, 0]), colify(f(conv_w)[0, 1]), colify(f(conv_w)[0, 2]),
                           colify(f(ret_gn)[0])], axis=1)
    cols = np.ascontiguousarray(cols, dtype=np.float32)
    shared = dict(cols=cols, conv_w_in=f(conv_w_in)[0], conv_w_out=f(conv_w_out)[0], ret_w_in=f(ret_w_in)[0],
                  ret_w_out=f(ret_w_out)[0], ffn_w_gate=f(ffn_w_gate), ffn_w_up=f(ffn_w_up), ffn_w_down=f(ffn_w_down), **consts)
    in_maps = []
    for c in range(8):
        m = dict(shared)
        m["xin"] = np.ascontiguousarray(np.concatenate([x_prompt[c], x_sample[16 * c:16 * c + 16].reshape(64, D)], axis=0))
        m["sconv"] = np.ascontiguousarray(state_conv[0, 16 * c:16 * c + 16].reshape(32, D))
        m["sret"] = np.ascontiguousarray(state_ret[0, 16 * c:16 * c + 16])
        in_maps.append(m)
    res = run_bass_kernel_spmd(nc, in_maps, core_ids=list(range(8)))
    R = res.results
    y_prompt = np.stack([R[c]["y"][:2048] for c in range(8)], axis=0)
    y_sample = np.concatenate([R[c]["y"][2048:].reshape(16, 4, D) for c in range(8)], axis=0)
    conv_prompt = np.stack([R[c]["convo"][0:2] for c in range(8)], axis=0)[None]
    conv_sample = np.concatenate([R[c]["convo"][2:34].reshape(16, 2, D) for c in range(8)], axis=0)[None]
    ret_prompt = np.stack([R[c]["retp"] for c in range(8)], axis=0)[None]
    ret_sample = np.concatenate([R[c]["rets"] for c in range(8)], axis=0)[None]
    return (y_prompt.astype(np.float32), y_sample.astype(np.float32), conv_prompt.astype(np.float32),
            conv_sample.astype(np.float32), ret_prompt.astype(np.float32), ret_sample.astype(np.float32))
```
